# Optimizing a Trainium2 kernel written in Bass

```python
import math
import jax, jax.numpy as jnp
from jax import lax
import numpy as np

D_MODEL = 1024
BATCH = 2
SEQ = 16384
DEPTH = 2

N_MIXERS = 4
GROUP_W = D_MODEL // N_MIXERS
D_MIX = N_MIXERS * GROUP_W

MLA_HEADS = 4
MLA_V_DIM = GROUP_W // MLA_HEADS
MLA_NOPE_DIM = MLA_V_DIM // 2
MLA_ROPE_DIM = MLA_V_DIM // 4
MLA_Q_LORA = GROUP_W
MLA_KV_LORA = GROUP_W // 2
ATTN_BLOCK = 128

SSD_HEADS = 4
SSD_HEAD_DIM = GROUP_W // SSD_HEADS
SSD_INNER = SSD_HEADS * SSD_HEAD_DIM
SSD_GROUPS = 2
SSD_STATE = 128
SSD_XBC = SSD_INNER + 2 * SSD_GROUPS * SSD_STATE
SSD_CONV = 4
SSD_CHUNK = 128

RET_HEADS = 4
RET_HEAD_DIM = GROUP_W // RET_HEADS
RET_CHUNK = 128

LRU_WIDTH = GROUP_W
LRU_BLOCKS = 4
LRU_BLOCK_DIM = LRU_WIDTH // LRU_BLOCKS
LRU_CONV = 4
LRU_C = 8.0

D_FF = ((8 * D_MODEL + 3 * 256 - 1) // (3 * 256)) * 256

ROPE_THETA = 10000.0
NORM_EPS = 1e-5
ALPHA = (2.0 * DEPTH) ** 0.25
BETA = (8.0 * DEPTH) ** -0.25

N_IN = (MLA_Q_LORA + MLA_KV_LORA + MLA_ROPE_DIM + SSD_INNER + SSD_XBC + SSD_HEADS
        + 4 * GROUP_W + 2 * LRU_WIDTH)

F32 = jnp.float32

kernel_name = "hybrid_mla_ssd_retention_rglru_block"


def _in_splits():
    widths = [MLA_Q_LORA, MLA_KV_LORA, MLA_ROPE_DIM,
              SSD_INNER, SSD_XBC, SSD_HEADS,
              GROUP_W, GROUP_W, GROUP_W, GROUP_W,
              LRU_WIDTH, LRU_WIDTH]
    return [int(v) for v in np.cumsum(widths)[:-1]]


def layernorm(x, g, b):
    xf = x.astype(F32)
    mu = jnp.mean(xf, -1, keepdims=True)
    var = jnp.mean(jnp.square(xf - mu), -1, keepdims=True)
    return ((xf - mu) * lax.rsqrt(var + NORM_EPS) * g + b).astype(x.dtype)


def rmsnorm(x, g):
    xf = x.astype(F32)
    return (xf * lax.rsqrt(jnp.mean(jnp.square(xf), -1, keepdims=True) + NORM_EPS) * g).astype(x.dtype)


def rope(x, positions):
    d = x.shape[-1]
    inv = ROPE_THETA ** (-jnp.arange(0, d, 2, dtype=F32) / d)
    ang = positions.astype(F32)[..., None] * inv
    cos = jnp.cos(ang)[:, :, None, :].astype(x.dtype)
    sin = jnp.sin(ang)[:, :, None, :].astype(x.dtype)
    x1, x2 = x[..., : d // 2], x[..., d // 2:]
    return jnp.concatenate([x1 * cos - x2 * sin, x1 * sin + x2 * cos], -1)


def causal_dwconv(x, w, b):
    k, c = w.shape
    y = lax.conv_general_dilated(x, w[:, None, :], window_strides=(1,), padding=[(k - 1, 0)],
                                 dimension_numbers=("NWC", "WIO", "NWC"), feature_group_count=c)
    return y + b


def mla_mixer(cq, ckv, kr, positions, g_q, w_uq, g_kv, w_ukv):
    b, s, _ = cq.shape
    dqk = MLA_NOPE_DIM + MLA_ROPE_DIM
    q = (rmsnorm(cq, g_q) @ w_uq).reshape(b, s, MLA_HEADS, dqk)
    q = jnp.concatenate([q[..., :MLA_NOPE_DIM], rope(q[..., MLA_NOPE_DIM:], positions)], -1)
    kv = (rmsnorm(ckv, g_kv) @ w_ukv).reshape(b, s, MLA_HEADS, MLA_NOPE_DIM + MLA_V_DIM)
    k_rope = jnp.broadcast_to(rope(kr[:, :, None, :], positions), (b, s, MLA_HEADS, MLA_ROPE_DIM))
    k = jnp.concatenate([kv[..., :MLA_NOPE_DIM], k_rope], -1)
    v = kv[..., MLA_NOPE_DIM:]
    scale = dqk ** -0.5
    nb = s // ATTN_BLOCK
    q_blocks = q.reshape(b, nb, ATTN_BLOCK, MLA_HEADS, dqk).transpose(1, 0, 2, 3, 4)
    k_pos = jnp.arange(s)

    def block(args):
        qb, i = args
        q_pos = i * ATTN_BLOCK + jnp.arange(ATTN_BLOCK)
        sc = jnp.einsum("bqhd,bkhd->bhqk", qb, k, preferred_element_type=F32) * scale
        sc = jnp.where(k_pos[None, :] <= q_pos[:, None], sc, -jnp.inf)
        p = jax.nn.softmax(sc, axis=-1).astype(v.dtype)
        return jnp.einsum("bhqk,bkhd->bqhd", p, v)

    o = lax.map(block, (q_blocks, jnp.arange(nb)))
    return o.transpose(1, 0, 2, 3, 4).reshape(b, s, MLA_HEADS * MLA_V_DIM)


def ssd_chunked(x, a, bm, cm):
    b, s, h, p = x.shape
    n = bm.shape[-1]
    c, l = s // SSD_CHUNK, SSD_CHUNK
    x = x.reshape(b, c, l, h, p)
    bm = bm.reshape(b, c, l, h, n)
    cm = cm.reshape(b, c, l, h, n)
    a_cs = jnp.cumsum(a.reshape(b, c, l, h).transpose(0, 3, 1, 2), -1)
    tril = jnp.tril(jnp.ones((l, l), dtype=bool))
    seg = jnp.exp(jnp.where(tril, a_cs[..., :, None] - a_cs[..., None, :], -jnp.inf))
    y_diag = jnp.einsum("bclhn,bcshn,bhcls,bcshp->bclhp", cm, bm, seg, x)
    decay_to_end = jnp.exp(a_cs[..., -1:] - a_cs)
    chunk_states = jnp.einsum("bclhn,bhcl,bclhp->bchpn", bm, decay_to_end, x)
    chunk_decay = jnp.exp(a_cs[..., -1])

    def step(state, inp):
        st, dec = inp
        return state * dec[..., None, None] + st, state

    _, prev = lax.scan(step, jnp.zeros((b, h, p, n), F32),
                       (chunk_states.transpose(1, 0, 2, 3, 4), chunk_decay.transpose(2, 0, 1)))
    prev = prev.transpose(1, 0, 2, 3, 4)
    y_off = jnp.einsum("bclhn,bchpn,bhcl->bclhp", cm, prev, jnp.exp(a_cs))
    return (y_diag + y_off).reshape(b, s, h, p)


def ssd_mixer(z, xbc, dt_raw, conv_w, conv_b, dt_bias, a_log, d_skip, norm_g):
    b, s, _ = z.shape
    xbc = jax.nn.silu(causal_dwconv(xbc, conv_w, conv_b)).astype(F32)
    gn = SSD_GROUPS * SSD_STATE
    rep = SSD_HEADS // SSD_GROUPS
    xs = xbc[..., :SSD_INNER].reshape(b, s, SSD_HEADS, SSD_HEAD_DIM)
    bm = jnp.repeat(xbc[..., SSD_INNER:SSD_INNER + gn].reshape(b, s, SSD_GROUPS, SSD_STATE), rep, axis=2)
    cm = jnp.repeat(xbc[..., SSD_INNER + gn:].reshape(b, s, SSD_GROUPS, SSD_STATE), rep, axis=2)
    dt = jax.nn.softplus(dt_raw.astype(F32) + dt_bias.astype(F32))
    a = -jnp.exp(a_log.astype(F32))
    y = ssd_chunked(xs * dt[..., None], dt * a, bm, cm) + xs * d_skip.astype(F32)[:, None]
    y = y.reshape(b, s, SSD_INNER) * jax.nn.silu(z.astype(F32))
    return rmsnorm(y, norm_g.astype(F32)).astype(z.dtype)


def retention_mixer(q, k, v, g, positions, gn_g, gn_b):
    b, s, _ = q.shape
    H, d, l = RET_HEADS, RET_HEAD_DIM, RET_CHUNK
    c = s // l
    q = rope(q.reshape(b, s, H, d), positions).astype(F32)
    k = rope(k.reshape(b, s, H, d), positions).astype(F32) * (d ** -0.5)
    v = v.reshape(b, s, H, d).astype(F32)
    log_gamma = jnp.log1p(-(2.0 ** (-5.0 - jnp.arange(H, dtype=F32))))
    idx = jnp.arange(l, dtype=F32)
    rel = idx[:, None] - idx[None, :]
    intra = jnp.where(rel >= 0, jnp.exp(log_gamma[:, None, None] * jnp.maximum(rel, 0.0)), 0.0)
    q_dec = jnp.exp(log_gamma[None, :] * (idx[:, None] + 1.0))[None, :, :, None]
    k_dec = jnp.exp(log_gamma[None, :] * (l - 1.0 - idx[:, None]))[None, :, :, None]
    c_dec = jnp.exp(log_gamma * l)[None, :, None, None]

    def to_chunks(t):
        return t.reshape(b, c, l, H, d).transpose(1, 0, 2, 3, 4)

    def step(state, inp):
        qc, kc, vc = inp
        sc = jnp.einsum("bihd,bjhd->bhij", qc, kc) * intra
        o = jnp.einsum("bhij,bjhe->bihe", sc, vc) + jnp.einsum("bihd,bhde->bihe", qc, state) * q_dec
        state = state * c_dec + jnp.einsum("bjhd,bjhe->bhde", kc * k_dec, vc)
        return state, o

    _, o = lax.scan(step, jnp.zeros((b, H, d, d), F32), (to_chunks(q), to_chunks(k), to_chunks(v)))
    o = o.transpose(1, 0, 2, 3, 4).reshape(b, s, H, d)
    mu = jnp.mean(o, -1, keepdims=True)
    var = jnp.mean(jnp.square(o - mu), -1, keepdims=True)
    o = ((o - mu) * lax.rsqrt(var + NORM_EPS)).reshape(b, s, H * d) * gn_g + gn_b
    return (jax.nn.silu(g.astype(F32)) * o).astype(g.dtype)


def _linear_recurrence_combine(left, right):
    a1, b1 = left
    a2, b2 = right
    return a1 * a2, a2 * b1 + b2


def rglru_mixer(xb, gb, conv_w, conv_b, w_a, b_a, w_x, b_x, a_param):
    b, s, _ = xb.shape
    u = causal_dwconv(xb, conv_w, conv_b).astype(F32)
    ub = u.reshape(b, s, LRU_BLOCKS, LRU_BLOCK_DIM)
    r = jax.nn.sigmoid(jnp.einsum("bsgi,gij->bsgj", ub, w_a.astype(F32)).reshape(b, s, LRU_WIDTH) + b_a)
    i = jax.nn.sigmoid(jnp.einsum("bsgi,gij->bsgj", ub, w_x.astype(F32)).reshape(b, s, LRU_WIDTH) + b_x)
    log_a = -LRU_C * r * jax.nn.softplus(-a_param.astype(F32))
    a = jnp.exp(log_a)
    inp = jnp.sqrt(-jnp.expm1(2.0 * log_a)) * (i * u)
    _, h = lax.associative_scan(_linear_recurrence_combine, (a, inp), axis=1)
    return (h * jax.nn.gelu(gb.astype(F32))).astype(xb.dtype)


def swiglu(x, w_in, w_out):
    gu = x @ w_in
    return (jax.nn.silu(gu[..., :D_FF]) * gu[..., D_FF:]) @ w_out


def setup_inputs(seed: int = 0) -> dict:
    key = jax.random.key(seed)
    ks = iter(jax.random.split(key, 40))
    L = DEPTH

    def nrm(shape, scale):
        return jax.random.normal(next(ks), shape, F32) * scale

    def gain(shape):
        return 1.0 + nrm(shape, 0.02)

    x = jax.random.normal(next(ks), (BATCH, SEQ, D_MODEL), F32)
    positions = jnp.broadcast_to(jnp.arange(SEQ, dtype=jnp.int32)[None, :], (BATCH, SEQ))
    w_in = nrm((L, D_MODEL, N_IN), D_MODEL ** -0.5)
    mla_g_q = gain((L, MLA_Q_LORA))
    mla_w_uq = nrm((L, MLA_Q_LORA, MLA_HEADS * (MLA_NOPE_DIM + MLA_ROPE_DIM)), MLA_Q_LORA ** -0.5)
    mla_g_kv = gain((L, MLA_KV_LORA))
    mla_w_ukv = nrm((L, MLA_KV_LORA, MLA_HEADS * (MLA_NOPE_DIM + MLA_V_DIM)), MLA_KV_LORA ** -0.5)
    ssd_conv_w = nrm((L, SSD_CONV, SSD_XBC), SSD_CONV ** -0.5)
    ssd_conv_b = nrm((L, SSD_XBC), 0.02)
    u = jax.random.uniform(next(ks), (L, SSD_HEADS), F32)
    dt0 = jnp.exp(u * (math.log(0.1) - math.log(0.001)) + math.log(0.001))
    ssd_dt_bias = dt0 + jnp.log(-jnp.expm1(-dt0))
    ssd_a_log = jnp.log(jax.random.uniform(next(ks), (L, SSD_HEADS), F32, 1.0, 16.0))
    ssd_d = gain((L, SSD_HEADS))
    ssd_norm_g = gain((L, SSD_INNER))
    ret_gn_g = gain((L, GROUP_W))
    ret_gn_b = nrm((L, GROUP_W), 0.02)
    lru_conv_w = nrm((L, LRU_CONV, LRU_WIDTH), LRU_CONV ** -0.5)
    lru_conv_b = nrm((L, LRU_WIDTH), 0.02)
    lru_w_a = nrm((L, LRU_BLOCKS, LRU_BLOCK_DIM, LRU_BLOCK_DIM), LRU_BLOCK_DIM ** -0.5)
    lru_b_a = nrm((L, LRU_WIDTH), 0.02)
    lru_w_x = nrm((L, LRU_BLOCKS, LRU_BLOCK_DIM, LRU_BLOCK_DIM), LRU_BLOCK_DIM ** -0.5)
    lru_b_x = nrm((L, LRU_WIDTH), 0.02)
    a_c = jax.random.uniform(next(ks), (L, LRU_WIDTH), F32, 0.9, 0.999)
    a_s = a_c ** (1.0 / LRU_C)
    lru_a_param = jnp.log(a_s) - jnp.log1p(-a_s)
    w_out = nrm((L, D_MIX, D_MODEL), BETA * D_MIX ** -0.5)
    ln1_g = gain((L, D_MODEL))
    ln1_b = nrm((L, D_MODEL), 0.02)
    w_ffn_in = nrm((L, D_MODEL, 2 * D_FF), D_MODEL ** -0.5)
    w_ffn_out = nrm((L, D_FF, D_MODEL), BETA * D_FF ** -0.5)
    ln2_g = gain((L, D_MODEL))
    ln2_b = nrm((L, D_MODEL), 0.02)
    return {"x": x, "positions": positions, "w_in": w_in,
            "mla_g_q": mla_g_q, "mla_w_uq": mla_w_uq, "mla_g_kv": mla_g_kv, "mla_w_ukv": mla_w_ukv,
            "ssd_conv_w": ssd_conv_w, "ssd_conv_b": ssd_conv_b, "ssd_dt_bias": ssd_dt_bias,
            "ssd_a_log": ssd_a_log, "ssd_d": ssd_d, "ssd_norm_g": ssd_norm_g,
            "ret_gn_g": ret_gn_g, "ret_gn_b": ret_gn_b,
            "lru_conv_w": lru_conv_w, "lru_conv_b": lru_conv_b, "lru_w_a": lru_w_a, "lru_b_a": lru_b_a,
            "lru_w_x": lru_w_x, "lru_b_x": lru_b_x, "lru_a_param": lru_a_param,
            "w_out": w_out, "ln1_g": ln1_g, "ln1_b": ln1_b,
            "w_ffn_in": w_ffn_in, "w_ffn_out": w_ffn_out, "ln2_g": ln2_g, "ln2_b": ln2_b}


def reference(x, positions, w_in, mla_g_q, mla_w_uq, mla_g_kv, mla_w_ukv,
              ssd_conv_w, ssd_conv_b, ssd_dt_bias, ssd_a_log, ssd_d, ssd_norm_g,
              ret_gn_g, ret_gn_b,
              lru_conv_w, lru_conv_b, lru_w_a, lru_b_a, lru_w_x, lru_b_x, lru_a_param,
              w_out, ln1_g, ln1_b, w_ffn_in, w_ffn_out, ln2_g, ln2_b):
    splits = _in_splits()
    for l in range(DEPTH):
        h = x @ w_in[l]
        cq, ckv, kr, z, xbc, dt_raw, rq, rk, rv, rg, lx, lg = jnp.split(h, splits, axis=-1)
        y_a = mla_mixer(cq, ckv, kr, positions, mla_g_q[l], mla_w_uq[l], mla_g_kv[l], mla_w_ukv[l])
        y_b = ssd_mixer(z, xbc, dt_raw, ssd_conv_w[l], ssd_conv_b[l], ssd_dt_bias[l], ssd_a_log[l],
                        ssd_d[l], ssd_norm_g[l])
        y_c = retention_mixer(rq, rk, rv, rg, positions, ret_gn_g[l], ret_gn_b[l])
        y_d = rglru_mixer(lx, lg, lru_conv_w[l], lru_conv_b[l], lru_w_a[l], lru_b_a[l],
                          lru_w_x[l], lru_b_x[l], lru_a_param[l])
        mix = jnp.concatenate([y_a, y_b, y_c, y_d], axis=-1) @ w_out[l]
        x = layernorm(ALPHA * x + mix, ln1_g[l], ln1_b[l])
        x = layernorm(ALPHA * x + swiglu(x, w_ffn_in[l], w_ffn_out[l]), ln2_g[l], ln2_b[l])
    return x
```

```python
import math
import numpy as np
from contextlib import ExitStack
import concourse.bass as bass
import concourse.mybir as mybir
from concourse.bass_utils import run_bass_kernel_spmd

F32 = mybir.dt.float32
BF16 = mybir.dt.bfloat16
I32 = mybir.dt.int32
AF = mybir.ActivationFunctionType
ALU = mybir.AluOpType

EPOCH = 16000


class Dep:
    __slots__ = ("w", "r", "sem", "cnt", "name", "root", "bg")

    def __init__(self, name=""):
        self.root = False
        self.bg = False
        self.w = None
        self.r = []
        self.sem = None
        self.cnt = 0
        self.name = name


class T:
    def __init__(self, ap, dep):
        self.ap = ap
        self.dep = dep

    def __getitem__(self, idx):
        return T(self.ap[idx], self.dep)

    def re(self, s, **kw):
        return T(self.ap.rearrange(s, **kw), self.dep)

    def bc(self, shape):
        return T(self.ap.to_broadcast(shape), self.dep)

    def wd(self, dep):
        return T(self.ap, dep)

    @property
    def shape(self):
        return self.ap.shape


class Sched:
    ENGS = ("pe", "act", "dve", "pool", "sp")

    def __init__(self):
        self.nc = bass.Bass("TRN2", target_bir_lowering=False)
        self.es = ExitStack()
        self.root_es = self.es
        self.prog = {e: [] for e in self.ENGS}
        self.count = {e: 0 for e in self.ENGS}
        self.esem = {e: [] for e in self.ENGS}
        self.known = {e: {} for e in self.ENGS}
        self.semobjs = {}
        self.nsem = 0
        self.out_deps = []
        self.same_engine_sync = True
        self.free_sems = []
        self.ccsem = None
        self.rvc = {}
        self.live_dma = []
        self.scopes = []
        self.cc_toks = []

    def new_sem(self, name):
        name = f"{name}_n{self.nsem}"
        s = self.root_es.enter_context(self.nc.semaphore(name))
        self.nsem += 1
        self.semobjs[id(s)] = s
        return s

    def rv(self, eng, key):
        ck = (id(eng), key)
        if ck not in self.rvc:
            pid = eng.partition_id()
            c = pid % 4
            val = {"c": c, "c64": c * 64, "g128": (c - (pid % 2)) * 64, "ctok": c * TOK}[key]
            self.rvc[ck] = eng.compute_val(val)
        return self.rvc[ck]

    def dma_sem(self, dep):
        if self.free_sems:
            sem, cnt = self.free_sems.pop()
        else:
            sem, cnt = self.new_sem("d%d" % self.nsem), 0
        dep.sem, dep.cnt = sem, cnt
        self.live_dma.append(dep)
        if self.scopes and not dep.root:
            self.scopes[-1][1].append(dep)

    def barrier(self):
        toks = []
        for e in self.ENGS:
            if self.count[e] > 0:
                idx = self.count[e] - 1
                toks.append((self.esem[e][idx // EPOCH], idx % EPOCH + 1))
        for d in self.live_dma:
            if d.cnt > 0 and not d.bg:
                toks.append((d.sem, d.cnt))
        toks += self.cc_toks
        for e in self.ENGS:
            waits = []
            for sem, val in toks:
                k = id(sem)
                if self.known[e].get(k, 0) < val:
                    self.known[e][k] = val
                    waits.append((sem, val))
            if waits:
                self.prog[e].append((waits, None, None))

    def open_scope(self):
        self.scopes.append((self.es, []))
        self.es = ExitStack()

    def close_scope(self):
        self.barrier()
        self.es.close()
        self.es, deps = self.scopes.pop()
        for d in deps:
            self.free_sems.append((d.sem, d.cnt))
            self.live_dma.remove(d)

    def dram(self, name, shape, dtype, kind):
        h = self.nc.dram_tensor(name, list(shape), dtype, kind=kind)
        d = Dep(name)
        d.root = True
        t = T(h.ap(), d)
        if kind == "ExternalOutput":
            self.out_deps.append(d)
        return t

    def sb(self, name, shape, dtype):
        self.uid = getattr(self, "uid", 0) + 1
        name = f"{name}_u{self.uid}"
        h = self.es.enter_context(self.nc.sbuf_tensor(name, list(shape), dtype))
        self.sb_bytes = getattr(self, "sb_bytes", 0) + int(np.prod(shape[1:])) * (4 if dtype in (F32, I32) else 2)
        return T(h[tuple(slice(None) for _ in shape)], Dep(name))

    def ps(self, name, shape=(128, 512), dtype=F32):
        self.uid = getattr(self, "uid", 0) + 1
        name = f"{name}_u{self.uid}"
        h = self.es.enter_context(self.nc.psum_tensor(name, list(shape), dtype))
        return T(h[tuple(slice(None) for _ in shape)], Dep(name))

    def _eng_token(self, e):
        idx = self.count[e]
        self.count[e] += 1
        ep = idx // EPOCH
        while len(self.esem[e]) <= ep:
            self.esem[e].append(self.new_sem(f"s_{e}_{len(self.esem[e])}"))
        sem = self.esem[e][ep]
        return (sem, idx % EPOCH + 1, e)

    def _collect(self, e, reads, writes, dma_dst=None):
        waits = {}

        def need(tok):
            if tok is None:
                return
            sem, val, src = tok
            if src == e and (e == "pe" or not self.same_engine_sync) and e != "sp":
                return
            k = id(sem)
            if self.known[e].get(k, 0) >= val:
                return
            if waits.get(k, (None, 0))[1] < val:
                waits[k] = (sem, val)

        for t in reads:
            need(t.dep.w)
        for t in writes:
            d = t.dep
            if not (dma_dst is d and d.w is not None and d.w[2] == "dma" and d.w[0] is d.sem and not d.r):
                need(d.w)
            for tok in d.r:
                need(tok)
        for k, (sem, val) in waits.items():
            self.known[e][k] = val
        return list(waits.values())

    def _commit(self, tok, reads, writes):
        for t in reads:
            t.dep.r.append(tok)
        for t in writes:
            t.dep.w = tok
            t.dep.r = []

    def op(self, e, fn, reads, writes):
        waits = self._collect(e, reads, writes)
        tok = self._eng_token(e)
        self.prog[e].append((waits, fn, (tok[0], 1)))
        self._commit(tok, reads, writes)

    def dma(self, out, in_, q="sp", dyn_in=None):
        d = out.dep
        if d.sem is None:
            self.dma_sem(d)
        waits = self._collect(q, [in_], [out], dma_dst=d)
        d.cnt += 16
        tok = (d.sem, d.cnt, "dma")
        o, i = out.ap, in_.ap
        if dyn_in is None:
            self.prog[q].append((waits, lambda eng: eng.dma_start(out=o, in_=i), (d.sem, 16)))
        else:
            self.prog[q].append((waits, lambda eng: eng.dma_start(out=o, in_=dyn_in(eng)), (d.sem, 16)))
        self._commit(tok, [in_], [out])

    def dram_internal(self, name, shape, dtype):
        h = self.nc.dram_tensor(name, list(shape), dtype)
        d = Dep(name)
        d.root = True
        return T(h.ap(), d)

    def collective(self, kind, out, in_, groups):
        if self.ccsem is None:
            self.ccsem = self.new_sem("cc")
            self.ccn = 0
        waits = self._collect("pool", [in_], [out])
        self.ccn += 1
        tok = (self.ccsem, self.ccn, "cc")
        self.cc_toks = [(self.ccsem, self.ccn)]
        o, i = out.ap, in_.ap
        self.prog["pool"].append((waits, lambda eng: eng.collective_compute(
            kind, ALU.bypass, replica_groups=groups, ins=[i.opt()], outs=[o.opt()]), (self.ccsem, 1)))
        self._commit(tok, [in_], [out])

    def mm(self, out, lhsT, rhs, start=True, stop=True):
        o, a, b = out.ap, lhsT.ap, rhs.ap
        self.op("pe", lambda eng: eng.matmul(o, a, b, start=start, stop=stop), [lhsT, rhs], [out])

    def transpose(self, out, in_, ident):
        o, a, b = out.ap, in_.ap, ident.ap
        self.op("pe", lambda eng: eng.transpose(o, a, b), [in_, ident], [out])

    def act(self, out, in_, func, bias=None, scale=None, e="act"):
        reads = [in_]
        kw = {}
        if bias is not None:
            if isinstance(bias, T):
                reads.append(bias)
                kw["bias"] = bias.ap
            else:
                kw["bias"] = bias
        if scale is not None:
            if isinstance(scale, T):
                reads.append(scale)
                kw["scale"] = scale.ap
            else:
                kw["scale"] = scale
        o, a = out.ap, in_.ap
        self.op("act", lambda eng: eng.activation(o, a, func, **kw), reads, [out])

    def tt(self, e, out, in0, in1, op):
        o, a, b = out.ap, in0.ap, in1.ap
        self.op(e, lambda eng: eng.tensor_tensor(o, a, b, op), [in0, in1], [out])

    def ts(self, e, out, in0, s1, op0, s2=None, op1=None):
        reads = [in0]
        a1 = s1
        if isinstance(s1, T):
            reads.append(s1)
            a1 = s1.ap
        a2 = s2
        if isinstance(s2, T):
            reads.append(s2)
            a2 = s2.ap
        o, a = out.ap, in0.ap
        if op1 is None:
            self.op(e, lambda eng: eng.tensor_scalar(o, a, a1, None, op0), reads, [out])
        else:
            self.op(e, lambda eng: eng.tensor_scalar(o, a, a1, a2, op0, op1), reads, [out])

    def stt(self, out, in0, scalar, in1, op0, op1):
        reads = [in0, in1]
        sc = scalar
        if isinstance(scalar, T):
            reads.append(scalar)
            sc = scalar.ap
        o, a, b = out.ap, in0.ap, in1.ap
        self.op("dve", lambda eng: eng.scalar_tensor_tensor(o, a, sc, b, op0, op1), reads, [out])

    def copy(self, e, out, in_):
        o, a = out.ap, in_.ap
        if e == "act":
            self.op("act", lambda eng: eng.copy(o, a), [in_], [out])
        else:
            self.op(e, lambda eng: eng.tensor_copy(o, a), [in_], [out])

    def memset(self, e, out, val):
        o = out.ap
        self.op(e, lambda eng: eng.memset(o, val), [], [out])

    def scan(self, out, d0, d1, init, op0=ALU.mult, op1=ALU.add):
        reads = [d0, d1]
        iv = init
        if isinstance(init, T):
            reads.append(init)
            iv = init.ap
        o, a, b = out.ap, d0.ap, d1.ap
        self.op("dve", lambda eng: eng.tensor_tensor_scan(o, a, b, iv, op0, op1), reads, [out])

    def finish(self):
        self.barrier()
        final_waits = []
        for d in self.out_deps:
            if d.sem is not None:
                final_waits.append((d.sem, d.cnt))
        self.prog["sp"].append((final_waits, None, None))
        nc = self.nc
        engmap = {"pe": "tensor", "act": "scalar", "dve": "vector", "pool": "gpsimd", "sp": "sync"}
        with nc.Block() as block:
            for e in self.ENGS:
                prog = self.prog[e]
                if not prog:
                    continue

                def body(eng, prog=prog):
                    for waits, fn, inc in prog:
                        for sem, val in waits:
                            eng.wait_ge(sem, val)
                        if fn is not None:
                            if inc[1] is None:
                                fn(eng).then_inc(inc[0])
                            else:
                                fn(eng).then_inc(inc[0], inc[1])

                getattr(block, engmap[e])(body)
        self.es.close()
        return nc


D_MODEL = 1024
SEQ = 16384
BATCH = 2
DEPTH = 2
D_FF = 2816
N_IN = 2964
NHP = 3584
ALPHA = (2.0 * DEPTH) ** 0.25
EPS = 1e-5
TOK = 4096


def _ring(S, name, n, shape, dtype, ps=False):
    return [S.ps(f"{name}{i}", shape, dtype) if ps else S.sb(f"{name}{i}", shape, dtype) for i in range(n)]


class Ring:
    def __init__(self, items):
        self.items = items
        self.i = 0

    def next(self):
        t = self.items[self.i % len(self.items)]
        self.i += 1
        return t


def chunks(t, n):
    return [T(t.ap[:, c, :], Dep(f"{t.dep.name}_{c}")) for c in range(n)]


def recip(S, out, in_):
    o, a = out.ap, in_.ap
    S.op("dve", lambda eng: eng.reciprocal(o, a), [in_], [out])


def layernorm_fm(S, xfc, vecs, gcol, bcol, ones_b, eps_col, psA, psB, scr, TB, xb=None):
    sqr, xbr, tmpr = scr["sq"], scr["xbt"], scr["tmp"]
    for c in range(8):
        sq = sqr.next()
        xt = xbr.next()
        S.act(sq, xfc[c], AF.Square)
        S.copy("pool", xt, xfc[c])
        S.mm(psA[:, :TB], ones_b, xt, c == 0, c == 7)
        S.mm(psB[:, :TB], ones_b, sq, c == 0, c == 7)
    mean, msq, var, rstd = scr["mean"], scr["msq"], scr["var"], scr["rstd"]
    S.ts("dve", mean, psA[:, :TB], 1.0 / 1024, ALU.mult)
    S.tt("dve", msq, mean, mean, ALU.mult)
    S.stt(var, psB[:, :TB], 1.0 / 1024, msq, ALU.mult, ALU.subtract)
    S.act(rstd, var, AF.Sqrt, bias=eps_col, scale=1.0)
    recip(S, rstd, rstd)
    for c in range(8):
        tmp = tmpr.next()
        S.tt("dve", tmp, xfc[c], mean, ALU.subtract)
        S.tt("pool", tmp, tmp, rstd, ALU.mult)
        S.act(xfc[c], tmp, AF.Identity, bias=vecs[:, bcol + c:bcol + c + 1], scale=vecs[:, gcol + c:gcol + c + 1])
        if xb is not None:
            S.copy("pool", xb[c], xfc[c])


def build_token_c():
    S = Sched()
    xT = S.dram("xT", [1024, TOK], F32, "ExternalInput")
    yT = S.dram("yT", [1024, TOK], F32, "ExternalInput")
    w_out = S.dram("w_out", [1024, 1024], F32, "ExternalInput")
    w_fi = S.dram("w_fi", [1024, 2 * D_FF], F32, "ExternalInput")
    w_fo = S.dram("w_fo", [D_FF, 1024], F32, "ExternalInput")
    vecs_d = S.dram("vecs", [128, 36], F32, "ExternalInput")
    outT = S.dram("outT", [1024, TOK], F32, "ExternalOutput")
    emit_token_c(S, xT, yT, w_out, w_fi, w_fo, vecs_d, outT, False)
    return S.finish()


def emit_token_c(S, xT, yT, w_out, w_fi, w_fo, vecs_d, outT, dyn, yget=None, xbs=None, wq="pool"):
    S.open_scope()
    TB = 256
    NB = TOK // TB

    wo = S.sb("wo", [128, 8, 1024], BF16)
    wfi = S.sb("wfi", [128, 8, 2 * D_FF], BF16)
    wfo = S.sb("wfo", [128, 22, 1024], BF16)
    vecs = S.sb("vecsb", [128, 36], F32)
    ones_b = S.sb("ones_b", [128, 128], BF16)
    eps_col = S.sb("eps_col", [128, 1], F32)
    S.dma(vecs, vecs_d)
    S.memset("dve", ones_b, 1.0)
    S.memset("dve", eps_col, EPS)
    for k in range(8):
        S.dma(wo[:, k, :], w_out[k * 128:(k + 1) * 128, :], q=wq)
    wfi_k = chunks(wfi, 8)
    for k in range(8):
        for j in range(4):
            S.dma(wfi_k[k][:, j * 1408:(j + 1) * 1408], w_fi[k * 128:(k + 1) * 128, j * 1408:(j + 1) * 1408], q=wq)
    for f in range(22):
        S.dma(wfo[:, f, :], w_fo[f * 128:(f + 1) * 128, :], q=wq)

    xf = chunks(S.sb("xf", [128, 8, TB], F32), 8)
    yb = chunks(S.sb("yb", [128, 8, TB], BF16), 8)
    ybf = chunks(S.sb("ybf", [128, 4, TB], F32), 4)
    xb = chunks(S.sb("xb", [128, 8, TB], BF16), 8)
    act = chunks(S.sb("act", [128, 22, TB], BF16), 22)
    scr = {
        "sq": Ring(_ring(S, "sq", 2, [128, TB], BF16)),
        "xbt": Ring(_ring(S, "xbt", 2, [128, TB], BF16)),
        "tmp": Ring(_ring(S, "tmp", 2, [128, TB], F32)),
        "mean": S.sb("mean", [128, TB], F32), "msq": S.sb("msq", [128, TB], F32),
        "var": S.sb("var", [128, TB], F32), "rstd": S.sb("rstd", [128, TB], F32),
    }
    sgr = Ring(_ring(S, "sg", 2, [128, TB], F32))
    rt = S.sb("rt", [128, TB], F32)
    pr = Ring(_ring(S, "pm", 6, [128, 512], F32, ps=True))
    psA = S.ps("psA", [128, 512], F32)
    psB = S.ps("psB", [128, 512], F32)

    for tb in range(NB):
        cs = slice(tb * TB, (tb + 1) * TB)
        for c in range(8):
            S.dma(xf[c], xT[c * 128:(c + 1) * 128, cs])
        def ysrc(k):
            if yget is not None:
                return dict(in_=yget(k, tb * TB, TB))
            if not dyn:
                return dict(in_=yT[k * 128:(k + 1) * 128, cs])
            return dict(in_=yT, dyn_in=lambda eng, k=k: yT.ap[k * 128:(k + 1) * 128,
                                                              bass.ds(S.rv(eng, "ctok") + tb * TB, TB)])
        for c in range(4):
            S.dma(yb[2 * c + 1], q="pool", **ysrc(2 * c + 1))
            S.dma(ybf[c], **ysrc(2 * c))
        for c in range(4):
            sq = scr["sq"].next()
            S.act(sq[64:128, :], ybf[c][64:128, :], AF.Square)
            S.mm(psA[:, :TB], ones_b[64:128, :], sq[64:128, :], c == 0, c == 3)
        S.act(rt, psA[:, :TB], AF.Sqrt, bias=eps_col, scale=1.0 / 256)
        recip(S, rt, rt)
        for c in range(4):
            S.copy("act", yb[2 * c][0:64, :], ybf[c][0:64, :])
            S.stt(yb[2 * c][64:128, :], ybf[c][64:128, :], vecs[64:128, 32 + c:33 + c], rt[64:128, :], ALU.mult, ALU.mult)
        for oc in range(8):
            ps = pr.next()
            for k in range(8):
                S.mm(ps[:, :TB], wo[:, k, oc * 128:(oc + 1) * 128], yb[k], k == 0, k == 7)
            S.stt(xf[oc], xf[oc], ALPHA, ps[:, :TB], ALU.mult, ALU.add)
        layernorm_fm(S, xf, vecs, 0, 8, ones_b, eps_col, psA, psB, scr, TB, xb=xb)
        for f in range(22):
            pg = pr.next()
            pu = pr.next()
            for k in range(8):
                S.mm(pg[:, :TB], wfi_k[k][:, f * 128:(f + 1) * 128], xb[k], k == 0, k == 7)
            for k in range(8):
                S.mm(pu[:, :TB], wfi_k[k][:, D_FF + f * 128:D_FF + (f + 1) * 128], xb[k], k == 0, k == 7)
            sg = sgr.next()
            S.act(sg, pg[:, :TB], AF.Silu)
            S.tt("dve", act[f], sg, pu[:, :TB], ALU.mult)
        for oc in range(8):
            ps = pr.next()
            for f in range(22):
                S.mm(ps[:, :TB], wfo[:, f, oc * 128:(oc + 1) * 128], act[f], f == 0, f == 21)
            S.stt(xf[oc], xf[oc], ALPHA, ps[:, :TB], ALU.mult, ALU.add)
        layernorm_fm(S, xf, vecs, 16, 24, ones_b, eps_col, psA, psB, scr, TB, xb=(xb if xbs is not None else None))
        for oc in range(8):
            S.dma(outT[oc * 128:(oc + 1) * 128, cs], xf[oc])
            if xbs is not None:
                S.dma(xbs[2 * oc, :, cs], xb[oc][0:64, :])
                S.dma(xbs[2 * oc + 1, :, cs], xb[oc][64:128, :])
    S.close_scope()


def build_token_a():
    S = Sched()
    xT = S.dram("xT", [1024, TOK], F32, "ExternalInput")
    w_in = S.dram("w_in", [1024, NHP], F32, "ExternalInput")
    hT = S.dram("hT", [NHP, TOK], F32, "ExternalOutput")
    emit_token_a(S, xT, w_in, hT)
    return S.finish()


def emit_token_a(S, xT, w_in, hT):
    S.open_scope()
    TB = min(512, TOK)
    NB = TOK // TB
    NHC = NHP // 128
    wi = chunks(S.sb("wi", [128, 8, NHP], BF16), 8)
    for k in range(8):
        for j in range(2):
            S.dma(wi[k][:, j * 1792:(j + 1) * 1792], w_in[k * 128:(k + 1) * 128, j * 1792:(j + 1) * 1792], q="pool")
    xbr = Ring([chunks(S.sb(f"xb{i}", [128, 8, TB], BF16), 8) for i in range(2)])
    hor = Ring(_ring(S, "ho", 4, [128, TB], F32))
    pr = Ring(_ring(S, "pm", 8, [128, 512], F32, ps=True))
    n = 0
    for tb in range(NB):
        cs = slice(tb * TB, (tb + 1) * TB)
        xb = xbr.next()
        for c in range(8):
            S.dma(xb[c], xT[c * 128:(c + 1) * 128, cs], q="pool")
        for hc in range(NHC):
            ps = pr.next()
            for k in range(8):
                S.mm(ps[:, :TB], wi[k][:, hc * 128:(hc + 1) * 128], xb[k], k == 0, k == 7)
            ho = hor.next()
            S.copy("act" if n % 2 == 0 else "dve", ho, ps[:, :TB])
            n += 1
            S.dma(hT[hc * 128:(hc + 1) * 128, cs], ho)
    S.close_scope()


R_CQ, R_CKV, R_KR, R_KRR, R_Z, R_X, R_B, R_C, R_DT = 0, 256, 384, 400, 416, 480, 544, 672, 800
R_RQ, R_RQR, R_RK, R_RKR, R_RV, R_RG, R_LX, R_LG, NR = 808, 872, 936, 1000, 1064, 1128, 1192, 1256, 1320
NRP = 1408
C_INTRA, C_QDEC, C_TRIU, C_NEG, C_ID, NCST = 33, 161, 289, 417, 545, 673
W_Q, W_QR, W_K, W_V, W_A, W_X, W_TRI, W_ID, W_ONES, NWTS = 0, 96, 192, 224, 288, 352, 416, 544, 672, 800
MLA_SCALE = 48 ** -0.5
SECT = {'mla', 'ret', 'ssd', 'lru'}


def build_mixer(seq=SEQ):
    S = Sched()
    hs = S.dram("hs", [NR, seq], F32, "ExternalInput")
    tabs = S.dram("tabs", [160, seq], F32, "ExternalInput")
    cst_d = S.dram("cst", [128, NCST], F32, "ExternalInput")
    wts_d = S.dram("wts", [128, NWTS], F32, "ExternalInput")
    yT = S.dram("yT", [256, seq], F32, "ExternalOutput")
    emit_mixer(S, hs, tabs, cst_d, wts_d, yT, seq, False)
    return S.finish()


HROW = {"cq0": (0, 0, R_CQ), "cq1": (128, 0, R_CQ + 128), "ckv": (256, 0, R_CKV), "kr": (384, 0, R_KR),
        "krr": (2964, 0, R_KRR), "z": (400, 64, R_Z), "x": (656, 64, R_X), "B": (912, -128, R_B),
        "C": (1168, -128, R_C), "dt": (1424, 1, R_DT), "rq": (1428, 64, R_RQ), "rqr": (2980, 64, R_RQR),
        "rk": (1684, 64, R_RK), "rkr": (3236, 64, R_RKR), "rv": (1940, 64, R_RV), "rg": (2196, 64, R_RG),
        "lx": (2452, 64, R_LX), "lg": (2708, 64, R_LG)}


def emit_mixer(S, hs, tabs, cst_d, wts_d, yT, seq, fused, proj=None, ycc=None, wq="pool"):
    S.open_scope()
    BT = 512
    NBLK = seq // BT
    QT_ = seq // 4

    def hload(dst, key, nrows, ta, tb_):
        base, stride, r_unf = HROW[key]
        if not fused:
            S.dma(dst, hs[r_unf:r_unf + nrows, ta:tb_])
            return
        col = 0
        t = ta
        while t < tb_:
            q = t // QT_
            te = min(tb_, (q + 1) * QT_)
            n = te - t
            lo = t - q * QT_

            def dyn(eng, q=q, lo=lo, n=n):
                if stride == -128:
                    off = S.rv(eng, "g128") + (q * NHP + base)
                elif stride == 1:
                    off = S.rv(eng, "c") + (q * NHP + base)
                else:
                    off = S.rv(eng, "c64") + (q * NHP + base)
                return hs.ap[bass.ds(off, nrows), lo:lo + n]
            if stride == 0:
                r0 = q * NHP + base
                S.dma(dst[:, col:col + n], hs[r0:r0 + nrows, lo:lo + n])
            else:
                S.dma(dst[:, col:col + n], hs, dyn_in=dyn)
            col += n
            t = te

    cst = S.sb("cstb", [128, NCST], F32)
    wts = S.sb("wtsb", [128, NWTS], BF16)
    S.dma(cst, cst_d)
    S.dma(wts, wts_d, q="pool")
    if proj is not None:
        xball, w_head_d = proj
        Wh = chunks(S.sb("Wh", [128, 8, NRP], BF16), 8)
        for k in range(8):
            S.dma(Wh[k], w_head_d[k * 128:(k + 1) * 128, :], q=wq)
        xbkr = Ring([chunks(S.sb(f"xbk{i}", [128, 8, 512], BF16), 8) for i in range(1)])
        evn = [0]
    col = lambda i, p0=0, p1=128: cst[p0:p1, i:i + 1]
    intraT = cst[:, C_INTRA:C_INTRA + 128]
    qdec = cst[0:64, C_QDEC:C_QDEC + 128]
    triu = cst[:, C_TRIU:C_TRIU + 128]
    negm = cst[:, C_NEG:C_NEG + 128]
    ident = cst[:, C_ID:C_ID + 128]
    tri01 = wts[:, W_TRI:W_TRI + 128]
    identb = wts[:, W_ID:W_ID + 128]
    ones_b = wts[:, W_ONES:W_ONES + 128]

    dcol = S.sb("dcol", [128, 8], F32)
    S.memset("dve", dcol[:, 0:1], EPS)
    S.memset("dve", dcol[:, 1:2], 1.0)
    S.memset("dve", dcol[:, 5:6], 0.0)
    eps_col, one_col = dcol[:, 0:1], dcol[:, 1:2]
    S.act(dcol[:, 2:3], col(19), AF.Exp)
    S.ts("dve", dcol[:, 2:3], dcol[:, 2:3], -1.0, ALU.mult)
    A_col = dcol[:, 2:3]
    S.act(dcol[0:64, 6:7], col(30, 0, 64), AF.Exp, scale=-1.0)
    S.act(dcol[0:64, 6:7], dcol[0:64, 6:7], AF.Ln, bias=one_col[0:64, :], scale=1.0)
    S.ts("dve", dcol[0:64, 4:5], dcol[0:64, 6:7], -8.0, ALU.mult)
    S.ts("dve", dcol[0:64, 3:4], dcol[0:64, 6:7], -16.0, ALU.mult)
    cneg, cneg2 = dcol[0:64, 4:5], dcol[0:64, 3:4]
    one11 = S.sb("one11", [1, 1], F32)
    S.memset("dve", one11, 1.0)

    KT = S.sb("KT", [48, seq], BF16)
    V1 = S.sb("V1", [128, seq // 128, 128], BF16)
    S.memset("pool", V1[:, :, 64:128], 1.0)
    KTd = [Dep(f"KT{i}") for i in range(NBLK)]
    V1d = [Dep(f"V1{i}") for i in range(NBLK)]
    RST = S.sb("RST", [64, 64], F32)
    RSTb = S.sb("RSTb", [64, 64], BF16)
    SST = S.sb("SST", [128, 64], F32)
    SSTb = S.sb("SSTb", [128, 64], BF16)
    S.memset("dve", RST, 0.0)
    S.memset("dve", RSTb, 0.0)
    S.memset("dve", SST, 0.0)
    S.memset("dve", SSTb, 0.0)
    Hr = Ring(_ring(S, "H", 2, [64, BT], F32))
    hstate = [dcol[0:64, 5:6]]

    NB = 2
    PB = 1 if proj is not None else 2
    mk = lambda name, shape, dt=F32, n=NB: Ring(_ring(S, name, n, shape, dt))
    CQr, CKVr = mk("CQ", [128, 2, BT], n=1), mk("CKV", [128, BT], n=PB)
    KRr, KRRr = mk("KR", [48, BT], n=1), mk("KRR", [48, BT], n=1)
    Zr, XCr, BCr, CCr = mk("Z", [64, BT], n=1), mk("XC", [64, BT + 3], n=PB), mk("BC", [128, BT + 3], n=PB), mk("CC", [128, BT + 3], n=PB)
    DTr = mk("DT", [1, BT], n=PB)
    RQr, RQRr, RKr, RKRr = (mk(n, [64, BT], n=1) for n in ("RQ", "RQR", "RK", "RKR"))
    RVr, RGr = mk("RV", [64, BT], n=PB), mk("RG", [64, BT], n=PB)
    LXr, LGr = mk("LX", [64, BT + 3], n=PB), mk("LG", [64, BT], n=1)
    TRr, TMr = mk("TR", [64, 2, BT], n=PB), mk("TM", [48, 2, BT], n=1)
    w1 = lambda name, shape, dt=F32: S.sb(name, shape, dt)
    sqb = Ring(_ring(S, "sqb", 2, [128, BT], BF16))
    rnorm = Ring(_ring(S, "rnorm", PB, [128, BT], F32))
    cqn = w1("cqn", [128, 2, BT], BF16)
    ckvn = w1("ckvn", [128, BT], BF16)
    QT = w1("QT", [48, BT], BF16)
    rt1, rt2 = w1("rt1", [48, BT]), w1("rt2", [48, BT])
    PTr = Ring(_ring(S, "PT", 3, [128, BT], BF16))
    rden = w1("rden", [128, BT])
    YA = Ring(_ring(S, "YA", 1, [64, BT], F32))
    f64 = Ring(_ring(S, "f64", 6, [64, BT], F32))
    f128 = Ring(_ring(S, "f128", 2, [128, BT], F32))
    QRb, QDb, KRb = w1("QRb", [64, BT], BF16), w1("QDb", [64, BT], BF16), w1("KRb", [64, BT], BF16)
    XS, XSr = w1("XS", [64, BT]), None
    BS, CS = w1("BS", [128, BT], BF16), w1("CS", [128, BT], BF16)
    Ub = w1("Ub", [64, BT], BF16)
    small = Ring(_ring(S, "small", 8, [128, 8], F32))
    sq128 = Ring(_ring(S, "sq128", 4, [128, 128], F32))
    sq128b = Ring(_ring(S, "sq128b", 6, [128, 128], BF16))
    tm64b = Ring(_ring(S, "tm64b", 4, [128, 64], BF16))
    tmA = Ring(_ring(S, "tmA", 4, [128, 64], BF16))
    scA = Ring(_ring(S, "scA", 2, [128, 128], BF16))
    Of, Ob, Osq = w1("Of", [64, BT]), w1("Ob", [64, BT], BF16), w1("Osq", [64, BT], BF16)
    YBr, YCr, YDr = mk("YB", [64, BT], n=1), mk("YC", [64, BT], n=1), mk("YD", [64, BT], n=1)

    psS = Ring(_ring(S, "psS", 2, [128, 512], F32, ps=True))
    psO = S.ps("psO")
    psGl = _ring(S, "psG", 3, [128, 512], F32, ps=True)
    psG = Ring(psGl)
    psA_ = psGl[0]
    psB = Ring(psGl[1:3])
    psY = S.ps("psY")
    psR_ = S.ps("psRet")

    def conv4(dst, src, wc, bc, p):
        S.ts("dve", dst, src[0:p, 3:BT + 3], col(wc + 3, 0, p), ALU.mult, col(bc, 0, p), ALU.add)
        for j in (2, 1, 0):
            S.stt(dst, src[0:p, j:BT + j], col(wc + j, 0, p), dst, ALU.mult, ALU.add)

    HPREV = []
    ydeps = [Dep(f"ys{i}") for i in range(NBLK)]
    for bi in range(NBLK):
        t0 = bi * BT
        if ycc is not None and bi > 0:
            ycc(bi - 1, ydeps[bi - 1])
        ts_ = slice(t0, t0 + BT)
        CQ, CKV, KR, KRR = CQr.next(), CKVr.next(), KRr.next(), KRRr.next()
        Z, XC, BC, CC, DT = Zr.next(), XCr.next(), BCr.next(), CCr.next(), DTr.next()
        RQ, RQR, RK, RKR, RV, RG = RQr.next(), RQRr.next(), RKr.next(), RKRr.next(), RVr.next(), RGr.next()
        LX, LG, TR, TM = LXr.next(), LGr.next(), TRr.next(), TMr.next()
        if proj is None:
            hload(CQ[:, 0, :], "cq0", 128, t0, t0 + BT)
            hload(CQ[:, 1, :], "cq1", 128, t0, t0 + BT)
            hload(CKV, "ckv", 128, t0, t0 + BT)
            hload(KR[32:48, :], "kr", 16, t0, t0 + BT)
            hload(KRR[32:48, :], "krr", 16, t0, t0 + BT)
            hload(Z, "z", 64, t0, t0 + BT)
            hload(DT, "dt", 1, t0, t0 + BT)
            for tl, key, p in ((XC, "x", 64), (BC, "B", 128), (CC, "C", 128), (LX, "lx", 64)):
                if bi == 0:
                    S.memset("pool", tl[:, 0:3], 0.0)
                    hload(tl[:, 3:BT + 3], key, p, 0, BT)
                else:
                    hload(tl, key, p, t0 - 3, t0 + BT)
            for tl, key in ((RQ, "rq"), (RQR, "rqr"), (RK, "rk"), (RKR, "rkr"), (RV, "rv"), (RG, "rg"), (LG, "lg")):
                hload(tl, key, 64, t0, t0 + BT)
        else:
            q_ = t0 // QT_
            lc = t0 - q_ * QT_
            xbk = xbkr.next()
            for kc in range(8):
                S.dma(xbk[kc][0:64, :], xball[2 * kc, q_ * 64:(q_ + 1) * 64, lc:lc + BT])
                S.dma(xbk[kc][64:128, :], xball[2 * kc + 1, q_ * 64:(q_ + 1) * 64, lc:lc + BT])

            def evac(dst, srcp):
                e = "act" if evn[0] % 2 == 0 else "dve"
                evn[0] += 1
                S.copy(e, dst, srcp)
            plan = [
                [(CQ[:, 0, :], 0, 128)], [(CQ[:, 1, :], 0, 128)], [(CKV, 0, 128)],
                [(DT, 0, 1), (KR[32:48, :], 32, 16), (KRR[32:48, :], 64, 16)],
                [(Z, 0, 64), (XC[:, 3:BT + 3], 64, 64)], [(BC[:, 3:BT + 3], 0, 128)], [(CC[:, 3:BT + 3], 0, 128)],
                [(RQ, 0, 64), (RQR, 64, 64)], [(RK, 0, 64), (RKR, 64, 64)], [(RV, 0, 64), (RG, 64, 64)],
                [(LX[:, 3:BT + 3], 0, 64), (LG, 64, 64)],
            ]
            for tl in (XC, BC, CC, LX):
                if bi == 0:
                    S.memset("pool", tl[:, 0:3], 0.0)
                else:
                    S.copy("dve", tl[:, 0:3], HPREV[0][:, BT:BT + 3])
                    HPREV.pop(0)
            for ch, parts in enumerate(plan):
                pp = psG.next()
                for k in range(8):
                    S.mm(pp, Wh[k][:, ch * 128:(ch + 1) * 128], xbk[k], k == 0, k == 7)
                for dst, p0, n in parts:
                    evac(dst, pp[p0:p0 + n, :])
            HPREV.extend([XC, BC, CC, LX])
        S.dma(TR[:, 0, :], tabs[0:64, ts_])
        S.dma(TR[:, 1, :], tabs[64:128, ts_])
        S.dma(TM[32:48, 0, :], tabs[128:144, ts_])
        S.dma(TM[32:48, 1, :], tabs[144:160, ts_])
        cosR, sinR = TR[:, 0, :], TR[:, 1, :]
        cosM, sinM = TM[32:48, 0, :], TM[32:48, 1, :]

        if 'mla' in SECT:
            pass
            pq = psG.next()
            for k in range(2):
                sq = sqb.next()
                S.act(sq, CQ[:, k, :], AF.Square)
                S.mm(pq, ones_b, sq, k == 0, k == 1)
            rq_ = rnorm.next()
            S.act(rq_, pq, AF.Sqrt, bias=eps_col, scale=1.0 / 256)
            recip(S, rq_, rq_)
            for k in range(2):
                S.stt(cqn[:, k, :], CQ[:, k, :], col(k), rq_, ALU.mult, ALU.mult)
            pk = psG.next()
            sq = sqb.next()
            S.act(sq, CKV, AF.Square)
            S.mm(pk, ones_b, sq, True, True)
            rk_ = rnorm.next()
            S.act(rk_, pk, AF.Sqrt, bias=eps_col, scale=1.0 / 128)
            recip(S, rk_, rk_)
            S.stt(ckvn, CKV, col(2), rk_, ALU.mult, ALU.mult)
            pq = psG.next()
            for k in range(2):
                S.mm(pq[0:48, :], wts[:, W_Q + 48 * k:W_Q + 48 * (k + 1)], cqn[:, k, :], k == 0, k == 1)
            pqr = psG.next()
            for k in range(2):
                S.mm(pqr[0:48, :], wts[:, W_QR + 48 * k:W_QR + 48 * (k + 1)], cqn[:, k, :], k == 0, k == 1)
            S.copy("act", QT[0:32, :], pq[0:32, :])
            S.tt("dve", rt1[32:48, :], pq[32:48, :], cosM, ALU.mult)
            S.tt("dve", rt2[32:48, :], pqr[32:48, :], sinM, ALU.mult)
            S.tt("pool", QT[32:48, :], rt1[32:48, :], rt2[32:48, :], ALU.add)
            KTb = T(KT.ap[:, ts_], KTd[bi])
            pk = psG.next()
            S.mm(pk[0:32, :], wts[:, W_K:W_K + 32], ckvn, True, True)
            S.copy("act", KTb[0:32, :], pk[0:32, :])
            S.tt("dve", rt1[32:48, :], KR[32:48, :], cosM, ALU.mult)
            S.tt("pool", rt2[32:48, :], KRR[32:48, :], sinM, ALU.mult)
            S.tt("dve", KTb[32:48, :], rt1[32:48, :], rt2[32:48, :], ALU.add)
            pv = psG.next()
            for j in range(4):
                S.mm(pv[:, j * 64:(j + 1) * 64], ckvn[:, j * 128:(j + 1) * 128], wts[:, W_V:W_V + 64], True, True)
            V1b = T(V1.ap[:, 4 * bi:4 * bi + 4, 0:64], V1d[bi])
            S.copy("act", V1b, pv[:, 0:256].re("p (j e) -> p j e", j=4))

        def sec_ret():
            yield
            if 'ret' not in SECT:
                return
            a1, a2 = f64.next(), f64.next()
            S.tt("dve", a1, RQ, cosR, ALU.mult)
            S.tt("pool", a2, RQR, sinR, ALU.mult)
            S.tt("dve", a1, a1, a2, ALU.add)
            S.copy("pool", QRb, a1)
            for j in range(4):
                S.tt("dve" if j % 2 == 0 else "pool", QDb[:, j * 128:(j + 1) * 128], a1[:, j * 128:(j + 1) * 128], qdec, ALU.mult)
            b1, b2 = f64.next(), f64.next()
            S.tt("dve", b1, RK, cosR, ALU.mult)
            S.tt("pool", b2, RKR, sinR, ALU.mult)
            S.tt("dve", b1, b1, b2, ALU.add)
            S.act(KRb, b1, AF.Copy, scale=0.125)
            if 'noretloop' not in SECT:
                for j in range(4):
                    cj = slice(j * 128, (j + 1) * 128)
                    pt = psA_
                    ptb = T(pt.ap.bitcast(BF16), pt.dep)
                    S.transpose(ptb[:, 0:64], KRb[:, cj], identb[0:64, 0:64])
                    KD = tmA.next()
                    S.ts("dve", KD, ptb[:, 0:64], col(32), ALU.mult)
                    S.transpose(pt[:, 256:320], RV[:, cj], ident[0:64, 0:64])
                    Vt = tmA.next()
                    S.copy("act", Vt, pt[:, 256:320])
                    S.mm(pt[:, 384:512], KRb[:, cj], QRb[:, cj], True, True)
                    yield
                    sc = scA.next()
                    S.tt("dve", sc, pt[:, 384:512], intraT, ALU.mult)
                    S.mm(psR_[0:64, cj], Vt, sc, True, False)
                    S.mm(psR_[0:64, cj], RSTb, QDb[:, cj], False, True)
                    yield
                    S.mm(pt[0:64, 128:192], KD, Vt, True, True)
                    S.stt(RST, RST, col(31, 0, 64), pt[0:64, 128:192], ALU.mult, ALU.add)
                    S.copy("pool", RSTb, RST)
                    yield
            if 'noretgn' not in SECT:
                S.copy("act", Of, psR_[0:64, :])
                S.copy("pool", Ob, Of)
                S.act(Osq, Of, AF.Square)
                p1, p2 = psR_, psA_
                S.mm(p1[0:64, :], ones_b[0:64, 0:64], Ob, True, True)
                S.mm(p2[0:64, :], ones_b[0:64, 0:64], Osq, True, True)
                mean, msq, var = f64.next(), f64.next(), f64.next()
                S.ts("dve", mean, p1[0:64, :], 1.0 / 64, ALU.mult)
                S.tt("pool", msq, mean, mean, ALU.mult)
                S.stt(var, p2[0:64, :], 1.0 / 64, msq, ALU.mult, ALU.subtract)
                S.act(var, var, AF.Sqrt, bias=eps_col[0:64, :], scale=1.0)
                recip(S, var, var)
                S.tt("dve", Of, Of, mean, ALU.subtract)
                S.tt("pool", Of, Of, var, ALU.mult)
                S.act(Of, Of, AF.Identity, bias=col(22, 0, 64), scale=col(21, 0, 64))
                sg = f64.next()
                S.act(sg, RG, AF.Silu)
                yc = YCr.next()
                S.tt("dve", yc, Of, sg, ALU.mult)
                S.dma(T(yT.ap[bi, 128:192, :], ydeps[bi]) if proj is not None else yT[128:192, ts_], yc)

        def sec_ssd():
            yield
            if 'ssd' not in SECT:
                return
            xa = f128.next()[0:64, :]
            conv4(xa, XC, 3, 7, 64)
            S.act(XS, xa, AF.Silu)
            ba = f128.next()
            conv4(ba, BC, 8, 12, 128)
            S.act(BS, ba, AF.Silu)
            ca = f128.next()
            conv4(ca, CC, 13, 17, 128)
            S.act(CS, ca, AF.Silu)
            S.act(DT, DT, AF.Exp, bias=col(18, 0, 1), scale=1.0)
            S.act(DT, DT, AF.Ln, bias=one_col[0:1, :], scale=1.0)
            pd = psB.next()
            for j in range(4):
                S.mm(pd[:, j:j + 1], DT[0:1, j * 128:(j + 1) * 128], one11, True, True)
            sm = small.next()
            dt_tm, a_tm, acs, nacs, wcol, dte = sm[:, 0:4], None, None, None, None, None
            S.copy("dve", dt_tm, pd[:, 0:4])
            sm2 = small.next()
            a_tm = sm2[:, 0:4]
            S.ts("dve", a_tm, dt_tm, A_col, ALU.mult)
            S.mm(pd[:, 8:12], triu, a_tm, True, True)
            nacs = sm2[:, 4:8]
            S.ts("dve", nacs, pd[:, 8:12], -1.0, ALU.mult)
            sm3 = small.next()
            dte, wcol = sm3[:, 0:4], sm3[:, 4:8]
            for j in range(4):
                cj = slice(j * 128, (j + 1) * 128)
                abc = sq128.next()
                S.copy("pool", abc, a_tm[:, j:j + 1].bc([128, 128]))
                pr = psB.next()
                S.mm(pr[:, 0:128], abc, triu, True, True)
                S.mm(pr[:, 128:256], abc, triu, True, False)
                S.mm(pr[:, 128:256], ident, negm, False, True)
                yield
                E = sq128.next()
                S.act(E, pr[:, 0:128], AF.Exp)
                seg = sq128b.next()
                S.act(seg, pr[:, 128:256], AF.Exp, bias=nacs[:, j:j + 1], scale=1.0)
                S.act(dte[:, j:j + 1], pr[:, 127:128], AF.Exp, bias=nacs[:, j:j + 1], scale=1.0)
                S.tt("dve", wcol[:, j:j + 1], dte[:, j:j + 1], dt_tm[:, j:j + 1], ALU.mult)
                S.mm(pr[:, 256:384], BS[:, cj], CS[:, cj], True, True)
                MT = sq128b.next()
                S.tt("dve", MT, pr[:, 256:384], seg, ALU.mult)
                S.transpose(pr[:, 384:448], XS[:, cj], ident[0:64, 0:64])
                yield
                XDT, XDD = tm64b.next(), tm64b.next()
                S.ts("dve", XDT, pr[:, 384:448], dt_tm[:, j:j + 1], ALU.mult)
                S.ts("dve", XDD, pr[:, 384:448], wcol[:, j:j + 1], ALU.mult)
                pb = psB.next()
                pbb = T(pb.ap.bitcast(BF16), pb.dep)
                S.transpose(pbb[:, 0:128], BS[:, cj], identb)
                yield
                Bt = sq128b.next()
                S.copy("act", Bt, pbb[:, 0:128])
                Cs = sq128b.next()
                S.tt("pool", Cs, CS[:, cj], E, ALU.mult)
                S.mm(psY[0:64, cj], XDT, MT, True, False)
                S.mm(psY[0:64, cj], SSTb, Cs, False, True)
                yield
                S.mm(pb[:, 256:320], Bt, XDD, True, True)
                S.stt(SST, SST, E[:, 127:128], pb[:, 256:320], ALU.mult, ALU.add)
                S.copy("pool", SSTb, SST)
                yield
            yb = YBr.next()
            S.stt(yb, XS, col(20, 0, 64), psY[0:64, :], ALU.mult, ALU.add)
            sz = f128.next()[0:64, :]
            S.act(sz, Z, AF.Silu)
            S.tt("dve", yb, yb, sz, ALU.mult)
            S.dma(T(yT.ap[bi, 64:128, :], ydeps[bi]) if proj is not None else yT[64:128, ts_], yb)

        def sec_lru():
            yield
            if 'lru' not in SECT:
                return
            U = f64.next()
            conv4(U, LX, 23, 27, 64)
            S.copy("pool", Ub, U)
            pr = psA_
            S.mm(pr[0:64, :], wts[0:64, W_A:W_A + 64], Ub, True, True)
            pi = psR_
            S.mm(pi[0:64, :], wts[0:64, W_X:W_X + 64], Ub, True, True)
            rr, ii = f64.next(), f64.next()
            S.act(rr, pr[0:64, :], AF.Sigmoid, bias=col(28, 0, 64), scale=1.0)
            S.act(ii, pi[0:64, :], AF.Sigmoid, bias=col(29, 0, 64), scale=1.0)
            aa, a2_ = f64.next(), f64.next()
            S.act(aa, rr, AF.Exp, scale=cneg)
            S.act(a2_, rr, AF.Exp, scale=cneg2)
            S.ts("dve", a2_, a2_, -1.0, ALU.mult, 1.0, ALU.add)
            S.act(a2_, a2_, AF.Sqrt)
            S.tt("pool", ii, ii, U, ALU.mult)
            S.tt("dve", ii, ii, a2_, ALU.mult)
            yield
            H = Hr.next()
            S.scan(H, aa, ii, hstate[0])
            hstate[0] = H[:, BT - 1:BT]
            g1, g2 = f64.next(), f64.next()
            S.act(g1, LG, AF.Square)
            S.ts("dve", g1, g1, 0.044715, ALU.mult, 1.0, ALU.add)
            S.tt("pool", g1, g1, LG, ALU.mult)
            S.act(g2, g1, AF.Sigmoid, scale=2.0 * math.sqrt(2.0 / math.pi))
            S.tt("pool", g2, g2, LG, ALU.mult)
            yd = YDr.next()
            S.tt("dve", yd, H, g2, ALU.mult)
            S.dma(T(yT.ap[bi, 192:256, :], ydeps[bi]) if proj is not None else yT[192:256, ts_], yd)
        def stream_a():
            yield from sec_ret()
            yield from sec_lru()
        gens = [stream_a(), sec_ssd()]

        def pump(n):
            for _ in range(n):
                for g in list(gens):
                    try:
                        next(g)
                        break
                    except StopIteration:
                        gens.remove(g)
                else:
                    return
                gens.append(gens.pop(0))
        if 'mla' in SECT:
            pass
            nkb = 4 * bi + 4

            def emit_scores(kb):
                j = kb - 4 * bi
                qs = 0 if j < 0 else j * 128
                Kk = T(KT.ap[:, kb * 128:(kb + 1) * 128], KTd[kb // 4])
                ps = psS.next()
                S.mm(ps[:, qs:BT], Kk, QT[:, qs:BT], True, True)
                return ps
            nxt = emit_scores(0)
            for kb in range(nkb):
                j = kb - 4 * bi
                qs = 0 if j < 0 else j * 128
                Vk = T(V1.ap[:, kb, :], V1d[kb // 4])
                ps = nxt
                if kb + 1 < nkb:
                    nxt = emit_scores(kb + 1)
                PT = PTr.next()
                S.act(PT[:, qs:BT], ps[:, qs:BT], AF.Exp, scale=MLA_SCALE)
                if j >= 0:
                    S.tt("pool", PT[:, qs:qs + 128], PT[:, qs:qs + 128], tri01, ALU.mult)
                S.mm(psO[:, qs:BT], Vk, PT[:, qs:BT], kb == 0, kb == nkb - 1)
                pump(2 if nkb < 24 else 1)
            recip(S, rden[64:128, :], psO[64:128, :])
            ya = YA.next()
            S.tt("dve", ya, psO[0:64, :], rden[64:128, :], ALU.mult)
            S.dma(T(yT.ap[bi, 0:64, :], ydeps[bi]) if proj is not None else yT[0:64, ts_], ya)

        while gens:
            pump(1)
    if ycc is not None:
        ycc(NBLK - 1, ydeps[NBLK - 1])
    S.close_scope()


def build_tables(ntok=TOK):
    S = Sched()
    posr = S.dram("posr", [128, ntok], I32, "ExternalInput")
    fr = S.dram("fr", [128, 4], F32, "ExternalInput")
    tabs = S.dram("tabs", [160, ntok], F32, "ExternalOutput")
    emit_tables(S, posr, fr, tabs, ntok)
    return S.finish()


def emit_tables(S, posr, fr, tabs, ntok):
    S.open_scope()
    frs = S.sb("frs", [128, 4], F32)
    S.dma(frs, fr)
    CH = min(2048, ntok)
    TWO_PI = 2.0 * math.pi
    pir = Ring(_ring(S, "pi", 2, [128, CH], I32))
    pf = S.sb("pf", [128, CH], F32)
    angr = Ring(_ring(S, "ang", 2, [128, CH], F32))
    kf = S.sb("kf", [128, CH], F32)
    ki = S.sb("ki", [128, CH], I32)
    for c0 in range(0, ntok, CH):
        pi_ = pir.next()
        S.dma(pi_, posr[:, c0:c0 + CH])
        S.copy("dve", pf, pi_)
        for ti, (r0, half, sbase) in enumerate(((0, 64, 64), (128, 16, 32))):
            ang = angr.next()
            S.ts("dve", ang, pf, frs[:, ti:ti + 1], ALU.mult, frs[:, 2 + ti:3 + ti], ALU.add)
            S.ts("dve", kf, ang, 1.0 / TWO_PI, ALU.mult)
            S.copy("dve", ki, kf)
            S.copy("dve", kf, ki)
            S.stt(ang, kf, -TWO_PI, ang, ALU.mult, ALU.add)
            S.ts("dve", kf, ang, math.pi, ALU.is_gt, -TWO_PI, ALU.mult)
            S.tt("dve", ang, ang, kf, ALU.add)
            S.ts("dve", kf, ang, -math.pi, ALU.is_lt, TWO_PI, ALU.mult)
            S.tt("dve", ang, ang, kf, ALU.add)
            S.ts("dve", ang, ang, math.pi, ALU.min, -math.pi, ALU.max)
            S.act(ang, ang, AF.Sin)
            S.dma(tabs[r0:r0 + half, c0:c0 + CH], ang[0:half, :])
            S.dma(tabs[r0 + half:r0 + 2 * half, c0:c0 + CH], ang[sbase:sbase + half, :])
    S.close_scope()


def colvec(v):
    return np.ascontiguousarray(np.asarray(v, np.float32).reshape(-1, 128).T)


def swap_halves(a, width):
    sh = a.shape
    a = a.reshape(sh[:-1] + (sh[-1] // width, 2, width // 2))
    return np.ascontiguousarray(a[..., ::-1, :].reshape(sh))


def make_w_in_aug(w_in_l):
    w = np.zeros((1024, NHP), np.float32)
    w[:, :N_IN] = w_in_l
    w[:, 2964:2980] = swap_halves(w_in_l[:, 384:400], 16)
    w[:, 2980:3236] = swap_halves(w_in_l[:, 1428:1684], 64)
    w[:, 3236:3492] = swap_halves(w_in_l[:, 1684:1940], 64)
    return w


def head_rows(c):
    g = c // 2
    rows = []
    rows += list(range(0, 256)) + list(range(256, 384)) + list(range(384, 400)) + list(range(2964, 2980))
    rows += list(range(400 + 64 * c, 464 + 64 * c))
    rows += list(range(656 + 64 * c, 720 + 64 * c))
    rows += list(range(912 + 128 * g, 1040 + 128 * g))
    rows += list(range(1168 + 128 * g, 1296 + 128 * g))
    rows += [1424 + c] + [3500 + i for i in range(7)]
    rows += list(range(1428 + 64 * c, 1492 + 64 * c))
    rows += list(range(2980 + 64 * c, 3044 + 64 * c))
    rows += list(range(1684 + 64 * c, 1748 + 64 * c))
    rows += list(range(3236 + 64 * c, 3300 + 64 * c))
    rows += list(range(1940 + 64 * c, 2004 + 64 * c))
    rows += list(range(2196 + 64 * c, 2260 + 64 * c))
    rows += list(range(2452 + 64 * c, 2516 + 64 * c))
    rows += list(range(2708 + 64 * c, 2772 + 64 * c))
    assert len(rows) == NR
    return np.array(rows)


def rep(v, n=128):
    return np.full((n,), v, np.float32)


def mixer_consts(p, l, c):
    g = c // 2
    cst = np.zeros((128, NCST), np.float32)
    cst[:, 0] = p["mla_g_q"][l, 0:128]
    cst[:, 1] = p["mla_g_q"][l, 128:256]
    cst[:, 2] = p["mla_g_kv"][l]
    xs = slice(64 * c, 64 * c + 64)
    bs = slice(256 + 128 * g, 384 + 128 * g)
    cs_ = slice(512 + 128 * g, 640 + 128 * g)
    for j in range(4):
        cst[0:64, 3 + j] = p["ssd_conv_w"][l, j, xs]
        cst[:, 8 + j] = p["ssd_conv_w"][l, j, bs]
        cst[:, 13 + j] = p["ssd_conv_w"][l, j, cs_]
        cst[0:64, 23 + j] = p["lru_conv_w"][l, j, xs]
    cst[0:64, 7] = p["ssd_conv_b"][l, xs]
    cst[:, 12] = p["ssd_conv_b"][l, bs]
    cst[:, 17] = p["ssd_conv_b"][l, cs_]
    cst[:, 18] = rep(p["ssd_dt_bias"][l, c])
    cst[:, 19] = rep(p["ssd_a_log"][l, c])
    cst[:, 20] = rep(p["ssd_d"][l, c])
    cst[0:64, 21] = p["ret_gn_g"][l, xs]
    cst[0:64, 22] = p["ret_gn_b"][l, xs]
    cst[0:64, 27] = p["lru_conv_b"][l, xs]
    cst[0:64, 28] = p["lru_b_a"][l, xs]
    cst[0:64, 29] = p["lru_b_x"][l, xs]
    cst[0:64, 30] = p["lru_a_param"][l, xs]
    lg = np.log1p(-(2.0 ** (-5.0 - c)))
    idx = np.arange(128, dtype=np.float64)
    cst[:, 31] = np.exp(lg * 128)
    cst[:, 32] = np.exp(lg * (127.0 - idx))
    rel = idx[None, :] - idx[:, None]
    cst[:, C_INTRA:C_INTRA + 128] = np.where(rel >= 0, np.exp(lg * np.maximum(rel, 0.0)), 0.0)
    cst[0:64, C_QDEC:C_QDEC + 128] = np.exp(lg * (idx + 1.0))[None, :]
    cst[:, C_TRIU:C_TRIU + 128] = (rel >= 0)
    cst[:, C_NEG:C_NEG + 128] = np.where(rel >= 0, 0.0, -30000.0)
    cst[:, C_ID:C_ID + 128] = np.eye(128)
    wts = np.zeros((128, NWTS), np.float32)
    wq = p["mla_w_uq"][l][:, 48 * c:48 * c + 48]
    wqr = np.zeros_like(wq)
    wqr[:, 32:48] = swap_halves(wq[:, 32:48], 16)
    for k in range(2):
        wts[:, W_Q + 48 * k:W_Q + 48 * (k + 1)] = wq[128 * k:128 * (k + 1)]
        wts[:, W_QR + 48 * k:W_QR + 48 * (k + 1)] = wqr[128 * k:128 * (k + 1)]
    wkv = p["mla_w_ukv"][l][:, 96 * c:96 * c + 96]
    wts[:, W_K:W_K + 32] = wkv[:, 0:32]
    wts[:, W_V:W_V + 64] = wkv[:, 32:96]
    wts[0:64, W_A:W_A + 64] = p["lru_w_a"][l, c]
    wts[0:64, W_X:W_X + 64] = p["lru_w_x"][l, c]
    wts[:, W_TRI:W_TRI + 128] = (rel >= 0)
    wts[:, W_ID:W_ID + 128] = np.eye(128)
    wts[:, W_ONES:W_ONES + 128] = 1.0
    return cst, wts


def table_freqs():
    fr = np.zeros((128, 4), np.float32)
    r = np.arange(64)
    inv = (10000.0 ** (-(2.0 * (r % 32)).astype(np.float32) / 64.0)).astype(np.float32)
    fr[0:64, 0] = inv
    fr[64:128, 0] = np.where(r < 32, -inv, inv)
    r = np.arange(16)
    inv = (10000.0 ** (-(2.0 * (r % 8)).astype(np.float32) / 16.0)).astype(np.float32)
    fr[0:16, 1] = inv
    fr[32:48, 1] = np.where(r < 8, -inv, inv)
    fr[0:64, 2] = math.pi / 2
    fr[0:16, 3] = math.pi / 2
    return fr


def _run(nc, in_maps):
    res = run_bass_kernel_spmd(nc, in_maps, core_ids=list(range(8)))
    return res.results


STAGE = 99


def w_out_perm(w_out_l):
    idx = np.empty(1024, np.int64)
    for k in range(8):
        c, h = k // 2, k % 2
        for p_ in range(128):
            idx[k * 128 + p_] = (2 * h + p_ // 64) * 256 + c * 64 + p_ % 64
    return np.ascontiguousarray(w_out_l[idx])


def c_vecs(p, l):
    v = np.zeros((128, 36), np.float32)
    v[:, 0:8], v[:, 8:16] = colvec(p["ln1_g"][l]), colvec(p["ln1_b"][l])
    v[:, 16:24], v[:, 24:32] = colvec(p["ln2_g"][l]), colvec(p["ln2_b"][l])
    for c in range(4):
        v[64:128, 32 + c] = p["ssd_norm_g"][l, 64 * c:64 * c + 64]
    return v


def make_w_head(waug, c):
    g = c // 2
    w = np.zeros((1024, NRP), np.float32)
    w[:, 0:256] = waug[:, 0:256]
    w[:, 256:384] = waug[:, 256:384]
    w[:, 384] = waug[:, 1424 + c]
    w[:, 384 + 32:384 + 48] = waug[:, 384:400]
    w[:, 384 + 64:384 + 80] = waug[:, 2964:2980]
    pairs = [(400 + 64 * c, 656 + 64 * c), None, None, (1428 + 64 * c, 2980 + 64 * c), (1684 + 64 * c, 3236 + 64 * c),
             (1940 + 64 * c, 2196 + 64 * c), (2452 + 64 * c, 2708 + 64 * c)]
    w[:, 640:768] = waug[:, 912 + 128 * g:1040 + 128 * g]
    w[:, 768:896] = waug[:, 1168 + 128 * g:1296 + 128 * g]
    for ch, pr in zip((4, 5, 6, 7, 8, 9, 10), pairs):
        if pr is None:
            continue
        w[:, ch * 128:ch * 128 + 64] = waug[:, pr[0]:pr[0] + 64]
        w[:, ch * 128 + 64:ch * 128 + 128] = waug[:, pr[1]:pr[1] + 64]
    return w


def build_fused(stage=99):
    S = Sched()
    seq, tok = SEQ, TOK
    nblk = seq // 512
    IN = "ExternalInput"
    xT = S.dram("xT", [1024, tok], F32, IN)
    posr = S.dram("posr", [128, seq], I32, IN)
    fr = S.dram("fr", [128, 4], F32, IN)
    w_head = [S.dram(f"w_head{l}", [1024, NRP], F32, IN) for l in range(DEPTH)]
    w_out = [S.dram(f"w_out{l}", [1024, 1024], F32, IN) for l in range(DEPTH)]
    w_fi = [S.dram(f"w_fi{l}", [1024, 2 * D_FF], F32, IN) for l in range(DEPTH)]
    w_fo = [S.dram(f"w_fo{l}", [D_FF, 1024], F32, IN) for l in range(DEPTH)]
    vecs = [S.dram(f"vecs{l}", [128, 36], F32, IN) for l in range(DEPTH)]
    cst = [S.dram(f"cst{l}", [128, NCST], F32, IN) for l in range(DEPTH)]
    wts = [S.dram(f"wts{l}", [128, NWTS], F32, IN) for l in range(DEPTH)]
    outT = S.dram("outT", [1024, tok], F32, "ExternalOutput")
    tabs = S.dram_internal("tabs", [160, seq], F32)
    xbs = S.dram_internal("xbs", [16, 64, tok], BF16)
    xball = S.dram_internal("xball", [16, 256, tok], BF16)
    ysend = S.dram_internal("ysend", [nblk, 256, 512], F32)
    yall = S.dram_internal("yall", [nblk, 1024, 512], F32)
    yloc = S.dram_internal("yloc", [nblk // 4, 1024, 512], F32)
    xmid = S.dram_internal("xmid", [1024, tok], F32)
    G = [[0, 1, 2, 3], [4, 5, 6, 7]]
    whb = [S.dram_internal(f"whb{l}", [1024, NRP], BF16) for l in range(DEPTH)]
    whb[0].dep.bg = True
    for j in range(16):
        S.dma(xbs[j], xT[64 * j:64 * j + 64, :], q="pool")
    for k in range(8):
        S.dma(whb[0][k * 128:(k + 1) * 128, :], w_head[0][k * 128:(k + 1) * 128, :], q="pool")
    for j in range(16):
        S.collective("AllGather", xball[j], xbs[j], G)
    wob = [S.dram_internal(f"wob{l}", [1024, 1024], BF16) for l in range(DEPTH)]
    wfib = [S.dram_internal(f"wfib{l}", [1024, 2 * D_FF], BF16) for l in range(DEPTH)]
    wfob = [S.dram_internal(f"wfob{l}", [D_FF, 1024], BF16) for l in range(DEPTH)]
    for t_ in whb + wob + wfib + wfob:
        t_.dep.bg = True
    for l in range(DEPTH):
        for k in range(8 if l > 0 else 0):
            S.dma(whb[l][k * 128:(k + 1) * 128, :], w_head[l][k * 128:(k + 1) * 128, :], q="pool")
        for k in range(8):
            S.dma(wob[l][k * 128:(k + 1) * 128, :], w_out[l][k * 128:(k + 1) * 128, :], q="pool")
        for k in range(8):
            for j in range(4):
                S.dma(wfib[l][k * 128:(k + 1) * 128, j * 1408:(j + 1) * 1408],
                      w_fi[l][k * 128:(k + 1) * 128, j * 1408:(j + 1) * 1408], q="pool")
        for f in range(22):
            S.dma(wfob[l][f * 128:(f + 1) * 128, :], w_fo[l][f * 128:(f + 1) * 128, :], q="pool")
    emit_tables(S, posr, fr, tabs, seq)
    xcur = xT
    for l in range(DEPTH):
        if l > 0:
            for j in range(16):
                S.collective("AllGather", xball[j], xbs[j], G)

        def ycc(j, dep):
            S.collective("AllGather", T(yall.ap[j], Dep(f"ya{j}")), T(ysend.ap[j], dep), G)
        emit_mixer(S, None, tabs, cst[l], wts[l], ysend, seq, False, proj=(xball, whb[l]), ycc=ycc, wq="sp")
        yall4 = yall.re("(q t) r c -> q t r c", q=4)
        yall_t = T(yall.ap, Dep("yall_g"))
        nq = (seq // 512) // 4
        for t_ in range(nq):
            S.dma(T(yloc.ap[t_:t_ + 1], yloc.dep), yall_t,
                  dyn_in=lambda eng, t_=t_: yall4.ap[bass.ds(S.rv(eng, "c"), 1), t_])
        last = l == DEPTH - 1
        xdst = outT if last else xmid
        emit_token_c(S, xcur, None, wob[l], wfib[l], wfob[l], vecs[l], xdst, False,
                     yget=lambda k, ts, n: yloc[ts // 512, k * 128:(k + 1) * 128, ts % 512:ts % 512 + n],
                     xbs=None if last else xbs, wq="sp")
        xcur = xdst
    return S.finish()


def kernel(**inputs):
    p = {k: np.asarray(v) for k, v in inputs.items()}
    x = p["x"].astype(np.float32, copy=False)
    pos = p["positions"].astype(np.int32, copy=False)
    cores = [(b, q) for b in range(BATCH) for q in range(4)]
    tsl = lambda q: slice(q * TOK, (q + 1) * TOK)
    shared = {"fr": table_freqs()}
    waug = []
    for l in range(DEPTH):
        waug.append(make_w_in_aug(p["w_in"][l]))
        shared[f"w_out{l}"] = w_out_perm(p["w_out"][l])
        shared[f"w_fi{l}"] = np.ascontiguousarray(p["w_ffn_in"][l])
        shared[f"w_fo{l}"] = np.ascontiguousarray(p["w_ffn_out"][l])
        shared[f"vecs{l}"] = c_vecs(p, l)
    posr = [np.ascontiguousarray(np.broadcast_to(pos[b][None, :], (128, SEQ))) for b in range(BATCH)]
    in_maps = []
    for b, k in cores:
        m = dict(shared)
        m["xT"] = np.ascontiguousarray(x[b, tsl(k)].T)
        m["posr"] = posr[b]
        for l in range(DEPTH):
            m[f"cst{l}"], m[f"wts{l}"] = mixer_consts(p, l, k)
            m[f"w_head{l}"] = make_w_head(waug[l], k)
        in_maps.append(m)
    r = _run(build_fused(), in_maps)
    out = np.empty((BATCH, SEQ, D_MODEL), np.float32)
    for i, (b, q) in enumerate(cores):
        out[b, tsl(q)] = r[i]["outT"].T
    return out
```

```python
import math
import numpy as np
from contextlib import ExitStack
import concourse.bass as bass
import concourse.mybir as mybir
from concourse.bass_utils import run_bass_kernel_spmd

F32 = mybir.dt.float32
BF16 = mybir.dt.bfloat16
I32 = mybir.dt.int32
AF = mybir.ActivationFunctionType
ALU = mybir.AluOpType

EPOCH = 16000


class Dep:
    __slots__ = ("w", "r", "sem", "cnt", "name", "root", "bg")

    def __init__(self, name=""):
        self.root = False
        self.bg = False
        self.w = None
        self.r = []
        self.sem = None
        self.cnt = 0
        self.name = name


class T:
    def __init__(self, ap, dep):
        self.ap = ap
        self.dep = dep

    def __getitem__(self, idx):
        return T(self.ap[idx], self.dep)

    def re(self, s, **kw):
        return T(self.ap.rearrange(s, **kw), self.dep)

    def bc(self, shape):
        return T(self.ap.to_broadcast(shape), self.dep)

    def wd(self, dep):
        return T(self.ap, dep)

    @property
    def shape(self):
        return self.ap.shape


class Sched:
    ENGS = ("pe", "act", "dve", "pool", "sp")

    def __init__(self):
        self.nc = bass.Bass("TRN2", target_bir_lowering=False)
        self.es = ExitStack()
        self.root_es = self.es
        self.prog = {e: [] for e in self.ENGS}
        self.count = {e: 0 for e in self.ENGS}
        self.esem = {e: [] for e in self.ENGS}
        self.known = {e: {} for e in self.ENGS}
        self.semobjs = {}
        self.nsem = 0
        self.out_deps = []
        self.same_engine_sync = True
        self.free_sems = []
        self.ccsem = None
        self.rvc = {}
        self.live_dma = []
        self.scopes = []
        self.cc_toks = []

    def new_sem(self, name):
        name = f"{name}_n{self.nsem}"
        s = self.root_es.enter_context(self.nc.semaphore(name))
        self.nsem += 1
        self.semobjs[id(s)] = s
        return s

    def rv(self, eng, key):
        ck = (id(eng), key)
        if ck not in self.rvc:
            pid = eng.partition_id()
            c = pid % 4
            val = {"c": c, "c64": c * 64, "g128": (c - (pid % 2)) * 64, "ctok": c * TOK}[key]
            self.rvc[ck] = eng.compute_val(val)
        return self.rvc[ck]

    def dma_sem(self, dep):
        if self.free_sems:
            sem, cnt = self.free_sems.pop()
        else:
            sem, cnt = self.new_sem("d%d" % self.nsem), 0
        dep.sem, dep.cnt = sem, cnt
        self.live_dma.append(dep)
        if self.scopes and not dep.root:
            self.scopes[-1][1].append(dep)

    def barrier(self):
        toks = []
        for e in self.ENGS:
            if self.count[e] > 0:
                idx = self.count[e] - 1
                toks.append((self.esem[e][idx // EPOCH], idx % EPOCH + 1))
        for d in self.live_dma:
            if d.cnt > 0 and not d.bg:
                toks.append((d.sem, d.cnt))
        toks += self.cc_toks
        for e in self.ENGS:
            waits = []
            for sem, val in toks:
                k = id(sem)
                if self.known[e].get(k, 0) < val:
                    self.known[e][k] = val
                    waits.append((sem, val))
            if waits:
                self.prog[e].append((waits, None, None))

    def open_scope(self):
        self.scopes.append((self.es, []))
        self.es = ExitStack()

    def close_scope(self):
        self.barrier()
        self.es.close()
        self.es, deps = self.scopes.pop()
        for d in deps:
            self.free_sems.append((d.sem, d.cnt))
            self.live_dma.remove(d)

    def dram(self, name, shape, dtype, kind):
        h = self.nc.dram_tensor(name, list(shape), dtype, kind=kind)
        d = Dep(name)
        d.root = True
        t = T(h.ap(), d)
        if kind == "ExternalOutput":
            self.out_deps.append(d)
        return t

    def sb(self, name, shape, dtype):
        self.uid = getattr(self, "uid", 0) + 1
        name = f"{name}_u{self.uid}"
        h = self.es.enter_context(self.nc.sbuf_tensor(name, list(shape), dtype))
        self.sb_bytes = getattr(self, "sb_bytes", 0) + int(np.prod(shape[1:])) * (4 if dtype in (F32, I32) else 2)
        return T(h[tuple(slice(None) for _ in shape)], Dep(name))

    def ps(self, name, shape=(128, 512), dtype=F32):
        self.uid = getattr(self, "uid", 0) + 1
        name = f"{name}_u{self.uid}"
        h = self.es.enter_context(self.nc.psum_tensor(name, list(shape), dtype))
        return T(h[tuple(slice(None) for _ in shape)], Dep(name))

    def _eng_token(self, e):
        idx = self.count[e]
        self.count[e] += 1
        ep = idx // EPOCH
        while len(self.esem[e]) <= ep:
            self.esem[e].append(self.new_sem(f"s_{e}_{len(self.esem[e])}"))
        sem = self.esem[e][ep]
        return (sem, idx % EPOCH + 1, e)

    def _collect(self, e, reads, writes, dma_dst=None):
        waits = {}

        def need(tok):
            if tok is None:
                return
            sem, val, src = tok
            if src == e and (e == "pe" or not self.same_engine_sync) and e != "sp":
                return
            k = id(sem)
            if self.known[e].get(k, 0) >= val:
                return
            if waits.get(k, (None, 0))[1] < val:
                waits[k] = (sem, val)

        for t in reads:
            need(t.dep.w)
        for t in writes:
            d = t.dep
            if not (dma_dst is d and d.w is not None and d.w[2] == "dma" and d.w[0] is d.sem and not d.r):
                need(d.w)
            for tok in d.r:
                need(tok)
        for k, (sem, val) in waits.items():
            self.known[e][k] = val
        return list(waits.values())

    def _commit(self, tok, reads, writes):
        for t in reads:
            t.dep.r.append(tok)
        for t in writes:
            t.dep.w = tok
            t.dep.r = []

    def op(self, e, fn, reads, writes):
        waits = self._collect(e, reads, writes)
        tok = self._eng_token(e)
        self.prog[e].append((waits, fn, (tok[0], 1)))
        self._commit(tok, reads, writes)

    def dma(self, out, in_, q="sp", dyn_in=None):
        d = out.dep
        if d.sem is None:
            self.dma_sem(d)
        waits = self._collect(q, [in_], [out], dma_dst=d)
        d.cnt += 16
        tok = (d.sem, d.cnt, "dma")
        o, i = out.ap, in_.ap
        if dyn_in is None:
            self.prog[q].append((waits, lambda eng: eng.dma_start(out=o, in_=i), (d.sem, 16)))
        else:
            self.prog[q].append((waits, lambda eng: eng.dma_start(out=o, in_=dyn_in(eng)), (d.sem, 16)))
        self._commit(tok, [in_], [out])

    def dram_internal(self, name, shape, dtype):
        h = self.nc.dram_tensor(name, list(shape), dtype)
        d = Dep(name)
        d.root = True
        return T(h.ap(), d)

    def collective(self, kind, out, in_, groups):
        if self.ccsem is None:
            self.ccsem = self.new_sem("cc")
            self.ccn = 0
        waits = self._collect("pool", [in_], [out])
        self.ccn += 1
        tok = (self.ccsem, self.ccn, "cc")
        self.cc_toks = [(self.ccsem, self.ccn)]
        o, i = out.ap, in_.ap
        self.prog["pool"].append((waits, lambda eng: eng.collective_compute(
            kind, ALU.bypass, replica_groups=groups, ins=[i.opt()], outs=[o.opt()]), (self.ccsem, 1)))
        self._commit(tok, [in_], [out])

    def mm(self, out, lhsT, rhs, start=True, stop=True):
        o, a, b = out.ap, lhsT.ap, rhs.ap
        self.op("pe", lambda eng: eng.matmul(o, a, b, start=start, stop=stop), [lhsT, rhs], [out])

    def transpose(self, out, in_, ident):
        o, a, b = out.ap, in_.ap, ident.ap
        self.op("pe", lambda eng: eng.transpose(o, a, b), [in_, ident], [out])

    def act(self, out, in_, func, bias=None, scale=None, e="act"):
        reads = [in_]
        kw = {}
        if bias is not None:
            if isinstance(bias, T):
                reads.append(bias)
                kw["bias"] = bias.ap
            else:
                kw["bias"] = bias
        if scale is not None:
            if isinstance(scale, T):
                reads.append(scale)
                kw["scale"] = scale.ap
            else:
                kw["scale"] = scale
        o, a = out.ap, in_.ap
        self.op("act", lambda eng: eng.activation(o, a, func, **kw), reads, [out])

    def tt(self, e, out, in0, in1, op):
        o, a, b = out.ap, in0.ap, in1.ap
        self.op(e, lambda eng: eng.tensor_tensor(o, a, b, op), [in0, in1], [out])

    def ts(self, e, out, in0, s1, op0, s2=None, op1=None):
        reads = [in0]
        a1 = s1
        if isinstance(s1, T):
            reads.append(s1)
            a1 = s1.ap
        a2 = s2
        if isinstance(s2, T):
            reads.append(s2)
            a2 = s2.ap
        o, a = out.ap, in0.ap
        if op1 is None:
            self.op(e, lambda eng: eng.tensor_scalar(o, a, a1, None, op0), reads, [out])
        else:
            self.op(e, lambda eng: eng.tensor_scalar(o, a, a1, a2, op0, op1), reads, [out])

    def stt(self, out, in0, scalar, in1, op0, op1):
        reads = [in0, in1]
        sc = scalar
        if isinstance(scalar, T):
            reads.append(scalar)
            sc = scalar.ap
        o, a, b = out.ap, in0.ap, in1.ap
        self.op("dve", lambda eng: eng.scalar_tensor_tensor(o, a, sc, b, op0, op1), reads, [out])

    def copy(self, e, out, in_):
        o, a = out.ap, in_.ap
        if e == "act":
            self.op("act", lambda eng: eng.copy(o, a), [in_], [out])
        else:
            self.op(e, lambda eng: eng.tensor_copy(o, a), [in_], [out])

    def memset(self, e, out, val):
        o = out.ap
        self.op(e, lambda eng: eng.memset(o, val), [], [out])

    def scan(self, out, d0, d1, init, op0=ALU.mult, op1=ALU.add):
        reads = [d0, d1]
        iv = init
        if isinstance(init, T):
            reads.append(init)
            iv = init.ap
        o, a, b = out.ap, d0.ap, d1.ap
        self.op("dve", lambda eng: eng.tensor_tensor_scan(o, a, b, iv, op0, op1), reads, [out])

    def finish(self):
        self.barrier()
        final_waits = []
        for d in self.out_deps:
            if d.sem is not None:
                final_waits.append((d.sem, d.cnt))
        self.prog["sp"].append((final_waits, None, None))
        nc = self.nc
        engmap = {"pe": "tensor", "act": "scalar", "dve": "vector", "pool": "gpsimd", "sp": "sync"}
        with nc.Block() as block:
            for e in self.ENGS:
                prog = self.prog[e]
                if not prog:
                    continue

                def body(eng, prog=prog):
                    for waits, fn, inc in prog:
                        for sem, val in waits:
                            eng.wait_ge(sem, val)
                        if fn is not None:
                            if inc[1] is None:
                                fn(eng).then_inc(inc[0])
                            else:
                                fn(eng).then_inc(inc[0], inc[1])

                getattr(block, engmap[e])(body)
        self.es.close()
        return nc


D_MODEL = 1024
SEQ = 16384
BATCH = 2
DEPTH = 2
D_FF = 2816
N_IN = 2964
NHP = 3584
ALPHA = (2.0 * DEPTH) ** 0.25
EPS = 1e-5
TOK = 4096


def _ring(S, name, n, shape, dtype, ps=False):
    return [S.ps(f"{name}{i}", shape, dtype) if ps else S.sb(f"{name}{i}", shape, dtype) for i in range(n)]


class Ring:
    def __init__(self, items):
        self.items = items
        self.i = 0

    def next(self):
        t = self.items[self.i % len(self.items)]
        self.i += 1
        return t


def chunks(t, n):
    return [T(t.ap[:, c, :], Dep(f"{t.dep.name}_{c}")) for c in range(n)]


def recip(S, out, in_):
    o, a = out.ap, in_.ap
    S.op("dve", lambda eng: eng.reciprocal(o, a), [in_], [out])


def layernorm_fm(S, xfc, vecs, gcol, bcol, ones_b, eps_col, psA, psB, scr, TB, xb=None):
    sqr, xbr, tmpr = scr["sq"], scr["xbt"], scr["tmp"]
    for c in range(8):
        sq = sqr.next()
        xt = xbr.next()
        S.act(sq, xfc[c], AF.Square)
        S.copy("pool", xt, xfc[c])
        S.mm(psA[:, :TB], ones_b, xt, c == 0, c == 7)
        S.mm(psB[:, :TB], ones_b, sq, c == 0, c == 7)
    mean, msq, var, rstd = scr["mean"], scr["msq"], scr["var"], scr["rstd"]
    S.ts("dve", mean, psA[:, :TB], 1.0 / 1024, ALU.mult)
    S.tt("dve", msq, mean, mean, ALU.mult)
    S.stt(var, psB[:, :TB], 1.0 / 1024, msq, ALU.mult, ALU.subtract)
    S.act(rstd, var, AF.Sqrt, bias=eps_col, scale=1.0)
    recip(S, rstd, rstd)
    for c in range(8):
        tmp = tmpr.next()
        S.tt("dve", tmp, xfc[c], mean, ALU.subtract)
        S.tt("pool", tmp, tmp, rstd, ALU.mult)
        S.act(xfc[c], tmp, AF.Identity, bias=vecs[:, bcol + c:bcol + c + 1], scale=vecs[:, gcol + c:gcol + c + 1])
        if xb is not None:
            S.copy("pool", xb[c], xfc[c])


def build_token_c():
    S = Sched()
    xT = S.dram("xT", [1024, TOK], F32, "ExternalInput")
    yT = S.dram("yT", [1024, TOK], F32, "ExternalInput")
    w_out = S.dram("w_out", [1024, 1024], F32, "ExternalInput")
    w_fi = S.dram("w_fi", [1024, 2 * D_FF], F32, "ExternalInput")
    w_fo = S.dram("w_fo", [D_FF, 1024], F32, "ExternalInput")
    vecs_d = S.dram("vecs", [128, 36], F32, "ExternalInput")
    outT = S.dram("outT", [1024, TOK], F32, "ExternalOutput")
    emit_token_c(S, xT, yT, w_out, w_fi, w_fo, vecs_d, outT, False)
    return S.finish()


def emit_token_c(S, xT, yT, w_out, w_fi, w_fo, vecs_d, outT, dyn, yget=None, xbs=None, wq="pool"):
    S.open_scope()
    TB = 256
    NB = TOK // TB

    wo = S.sb("wo", [128, 8, 1024], BF16)
    wfi = S.sb("wfi", [128, 8, 2 * D_FF], BF16)
    wfo = S.sb("wfo", [128, 22, 1024], BF16)
    vecs = S.sb("vecsb", [128, 36], F32)
    ones_b = S.sb("ones_b", [128, 128], BF16)
    eps_col = S.sb("eps_col", [128, 1], F32)
    S.dma(vecs, vecs_d)
    S.memset("dve", ones_b, 1.0)
    S.memset("dve", eps_col, EPS)
    for k in range(8):
        S.dma(wo[:, k, :], w_out[k * 128:(k + 1) * 128, :], q=wq)
    wfi_k = chunks(wfi, 8)
    for k in range(8):
        for j in range(4):
            S.dma(wfi_k[k][:, j * 1408:(j + 1) * 1408], w_fi[k * 128:(k + 1) * 128, j * 1408:(j + 1) * 1408], q=wq)
    for f in range(22):
        S.dma(wfo[:, f, :], w_fo[f * 128:(f + 1) * 128, :], q=wq)

    xf = chunks(S.sb("xf", [128, 8, TB], F32), 8)
    yb = chunks(S.sb("yb", [128, 8, TB], BF16), 8)
    ybf = chunks(S.sb("ybf", [128, 4, TB], F32), 4)
    xb = chunks(S.sb("xb", [128, 8, TB], BF16), 8)
    act = chunks(S.sb("act", [128, 22, TB], BF16), 22)
    scr = {
        "sq": Ring(_ring(S, "sq", 2, [128, TB], BF16)),
        "xbt": Ring(_ring(S, "xbt", 2, [128, TB], BF16)),
        "tmp": Ring(_ring(S, "tmp", 2, [128, TB], F32)),
        "mean": S.sb("mean", [128, TB], F32), "msq": S.sb("msq", [128, TB], F32),
        "var": S.sb("var", [128, TB], F32), "rstd": S.sb("rstd", [128, TB], F32),
    }
    sgr = Ring(_ring(S, "sg", 2, [128, TB], F32))
    rt = S.sb("rt", [128, TB], F32)
    pr = Ring(_ring(S, "pm", 6, [128, 512], F32, ps=True))
    psA = S.ps("psA", [128, 512], F32)
    psB = S.ps("psB", [128, 512], F32)

    for tb in range(NB):
        cs = slice(tb * TB, (tb + 1) * TB)
        for c in range(8):
            S.dma(xf[c], xT[c * 128:(c + 1) * 128, cs])
        def ysrc(k):
            if yget is not None:
                return dict(in_=yget(k, tb * TB, TB))
            if not dyn:
                return dict(in_=yT[k * 128:(k + 1) * 128, cs])
            return dict(in_=yT, dyn_in=lambda eng, k=k: yT.ap[k * 128:(k + 1) * 128,
                                                              bass.ds(S.rv(eng, "ctok") + tb * TB, TB)])
        for c in range(4):
            S.dma(yb[2 * c + 1], q="pool", **ysrc(2 * c + 1))
            S.dma(ybf[c], **ysrc(2 * c))
        for c in range(4):
            sq = scr["sq"].next()
            S.act(sq[64:128, :], ybf[c][64:128, :], AF.Square)
            S.mm(psA[:, :TB], ones_b[64:128, :], sq[64:128, :], c == 0, c == 3)
        S.act(rt, psA[:, :TB], AF.Sqrt, bias=eps_col, scale=1.0 / 256)
        recip(S, rt, rt)
        for c in range(4):
            S.copy("act", yb[2 * c][0:64, :], ybf[c][0:64, :])
            S.stt(yb[2 * c][64:128, :], ybf[c][64:128, :], vecs[64:128, 32 + c:33 + c], rt[64:128, :], ALU.mult, ALU.mult)
        for oc in range(8):
            ps = pr.next()
            for k in range(8):
                S.mm(ps[:, :TB], wo[:, k, oc * 128:(oc + 1) * 128], yb[k], k == 0, k == 7)
            S.stt(xf[oc], xf[oc], ALPHA, ps[:, :TB], ALU.mult, ALU.add)
        layernorm_fm(S, xf, vecs, 0, 8, ones_b, eps_col, psA, psB, scr, TB, xb=xb)
        for f in range(22):
            pg = pr.next()
            pu = pr.next()
            for k in range(8):
                S.mm(pg[:, :TB], wfi_k[k][:, f * 128:(f + 1) * 128], xb[k], k == 0, k == 7)
            for k in range(8):
                S.mm(pu[:, :TB], wfi_k[k][:, D_FF + f * 128:D_FF + (f + 1) * 128], xb[k], k == 0, k == 7)
            sg = sgr.next()
            S.act(sg, pg[:, :TB], AF.Silu)
            S.tt("dve", act[f], sg, pu[:, :TB], ALU.mult)
        for oc in range(8):
            ps = pr.next()
            for f in range(22):
                S.mm(ps[:, :TB], wfo[:, f, oc * 128:(oc + 1) * 128], act[f], f == 0, f == 21)
            S.stt(xf[oc], xf[oc], ALPHA, ps[:, :TB], ALU.mult, ALU.add)
        layernorm_fm(S, xf, vecs, 16, 24, ones_b, eps_col, psA, psB, scr, TB, xb=(xb if xbs is not None else None))
        for oc in range(8):
            S.dma(outT[oc * 128:(oc + 1) * 128, cs], xf[oc])
            if xbs is not None:
                S.dma(xbs[2 * oc, :, cs], xb[oc][0:64, :])
                S.dma(xbs[2 * oc + 1, :, cs], xb[oc][64:128, :])
    S.close_scope()


def build_token_a():
    S = Sched()
    xT = S.dram("xT", [1024, TOK], F32, "ExternalInput")
    w_in = S.dram("w_in", [1024, NHP], F32, "ExternalInput")
    hT = S.dram("hT", [NHP, TOK], F32, "ExternalOutput")
    emit_token_a(S, xT, w_in, hT)
    return S.finish()


def emit_token_a(S, xT, w_in, hT):
    S.open_scope()
    TB = min(512, TOK)
    NB = TOK // TB
    NHC = NHP // 128
    wi = chunks(S.sb("wi", [128, 8, NHP], BF16), 8)
    for k in range(8):
        for j in range(2):
            S.dma(wi[k][:, j * 1792:(j + 1) * 1792], w_in[k * 128:(k + 1) * 128, j * 1792:(j + 1) * 1792], q="pool")
    xbr = Ring([chunks(S.sb(f"xb{i}", [128, 8, TB], BF16), 8) for i in range(2)])
    hor = Ring(_ring(S, "ho", 4, [128, TB], F32))
    pr = Ring(_ring(S, "pm", 8, [128, 512], F32, ps=True))
    n = 0
    for tb in range(NB):
        cs = slice(tb * TB, (tb + 1) * TB)
        xb = xbr.next()
        for c in range(8):
            S.dma(xb[c], xT[c * 128:(c + 1) * 128, cs], q="pool")
        for hc in range(NHC):
            ps = pr.next()
            for k in range(8):
                S.mm(ps[:, :TB], wi[k][:, hc * 128:(hc + 1) * 128], xb[k], k == 0, k == 7)
            ho = hor.next()
            S.copy("act" if n % 2 == 0 else "dve", ho, ps[:, :TB])
            n += 1
            S.dma(hT[hc * 128:(hc + 1) * 128, cs], ho)
    S.close_scope()


R_CQ, R_CKV, R_KR, R_KRR, R_Z, R_X, R_B, R_C, R_DT = 0, 256, 384, 400, 416, 480, 544, 672, 800
R_RQ, R_RQR, R_RK, R_RKR, R_RV, R_RG, R_LX, R_LG, NR = 808, 872, 936, 1000, 1064, 1128, 1192, 1256, 1320
NRP = 1408
C_INTRA, C_QDEC, C_TRIU, C_NEG, C_ID, NCST = 33, 161, 289, 417, 545, 673
W_Q, W_QR, W_K, W_V, W_A, W_X, W_TRI, W_ID, W_ONES, NWTS = 0, 96, 192, 224, 288, 352, 416, 544, 672, 800
MLA_SCALE = 48 ** -0.5
SECT = {'mla', 'ret', 'ssd', 'lru'}


def build_mixer(seq=SEQ):
    S = Sched()
    hs = S.dram("hs", [NR, seq], F32, "ExternalInput")
    tabs = S.dram("tabs", [160, seq], F32, "ExternalInput")
    cst_d = S.dram("cst", [128, NCST], F32, "ExternalInput")
    wts_d = S.dram("wts", [128, NWTS], F32, "ExternalInput")
    yT = S.dram("yT", [256, seq], F32, "ExternalOutput")
    emit_mixer(S, hs, tabs, cst_d, wts_d, yT, seq, False)
    return S.finish()


HROW = {"cq0": (0, 0, R_CQ), "cq1": (128, 0, R_CQ + 128), "ckv": (256, 0, R_CKV), "kr": (384, 0, R_KR),
        "krr": (2964, 0, R_KRR), "z": (400, 64, R_Z), "x": (656, 64, R_X), "B": (912, -128, R_B),
        "C": (1168, -128, R_C), "dt": (1424, 1, R_DT), "rq": (1428, 64, R_RQ), "rqr": (2980, 64, R_RQR),
        "rk": (1684, 64, R_RK), "rkr": (3236, 64, R_RKR), "rv": (1940, 64, R_RV), "rg": (2196, 64, R_RG),
        "lx": (2452, 64, R_LX), "lg": (2708, 64, R_LG)}


def emit_mixer(S, hs, tabs, cst_d, wts_d, yT, seq, fused, proj=None, ycc=None, wq="pool"):
    S.open_scope()
    BT = 512
    NBLK = seq // BT
    QT_ = seq // 4

    def hload(dst, key, nrows, ta, tb_):
        base, stride, r_unf = HROW[key]
        if not fused:
            S.dma(dst, hs[r_unf:r_unf + nrows, ta:tb_])
            return
        col = 0
        t = ta
        while t < tb_:
            q = t // QT_
            te = min(tb_, (q + 1) * QT_)
            n = te - t
            lo = t - q * QT_

            def dyn(eng, q=q, lo=lo, n=n):
                if stride == -128:
                    off = S.rv(eng, "g128") + (q * NHP + base)
                elif stride == 1:
                    off = S.rv(eng, "c") + (q * NHP + base)
                else:
                    off = S.rv(eng, "c64") + (q * NHP + base)
                return hs.ap[bass.ds(off, nrows), lo:lo + n]
            if stride == 0:
                r0 = q * NHP + base
                S.dma(dst[:, col:col + n], hs[r0:r0 + nrows, lo:lo + n])
            else:
                S.dma(dst[:, col:col + n], hs, dyn_in=dyn)
            col += n
            t = te

    cst = S.sb("cstb", [128, NCST], F32)
    wts = S.sb("wtsb", [128, NWTS], BF16)
    S.dma(cst, cst_d)
    S.dma(wts, wts_d, q="pool")
    if proj is not None:
        xball, w_head_d = proj
        Wh = chunks(S.sb("Wh", [128, 8, NRP], BF16), 8)
        for k in range(8):
            S.dma(Wh[k], w_head_d[k * 128:(k + 1) * 128, :], q=wq)
        xbkr = Ring([chunks(S.sb(f"xbk{i}", [128, 8, 512], BF16), 8) for i in range(1)])
        evn = [0]
    col = lambda i, p0=0, p1=128: cst[p0:p1, i:i + 1]
    intraT = cst[:, C_INTRA:C_INTRA + 128]
    qdec = cst[0:64, C_QDEC:C_QDEC + 128]
    triu = cst[:, C_TRIU:C_TRIU + 128]
    negm = cst[:, C_NEG:C_NEG + 128]
    ident = cst[:, C_ID:C_ID + 128]
    tri01 = wts[:, W_TRI:W_TRI + 128]
    identb = wts[:, W_ID:W_ID + 128]
    ones_b = wts[:, W_ONES:W_ONES + 128]

    dcol = S.sb("dcol", [128, 8], F32)
    S.memset("dve", dcol[:, 0:1], EPS)
    S.memset("dve", dcol[:, 1:2], 1.0)
    S.memset("dve", dcol[:, 5:6], 0.0)
    eps_col, one_col = dcol[:, 0:1], dcol[:, 1:2]
    S.act(dcol[:, 2:3], col(19), AF.Exp)
    S.ts("dve", dcol[:, 2:3], dcol[:, 2:3], -1.0, ALU.mult)
    A_col = dcol[:, 2:3]
    S.act(dcol[0:64, 6:7], col(30, 0, 64), AF.Exp, scale=-1.0)
    S.act(dcol[0:64, 6:7], dcol[0:64, 6:7], AF.Ln, bias=one_col[0:64, :], scale=1.0)
    S.ts("dve", dcol[0:64, 4:5], dcol[0:64, 6:7], -8.0, ALU.mult)
    S.ts("dve", dcol[0:64, 3:4], dcol[0:64, 6:7], -16.0, ALU.mult)
    cneg, cneg2 = dcol[0:64, 4:5], dcol[0:64, 3:4]
    one11 = S.sb("one11", [1, 1], F32)
    S.memset("dve", one11, 1.0)

    KT = S.sb("KT", [48, seq], BF16)
    V1 = S.sb("V1", [128, seq // 128, 128], BF16)
    S.memset("pool", V1[:, :, 64:128], 1.0)
    KTd = [Dep(f"KT{i}") for i in range(NBLK)]
    V1d = [Dep(f"V1{i}") for i in range(NBLK)]
    RST = S.sb("RST", [64, 64], F32)
    RSTb = S.sb("RSTb", [64, 64], BF16)
    SST = S.sb("SST", [128, 64], F32)
    SSTb = S.sb("SSTb", [128, 64], BF16)
    S.memset("dve", RST, 0.0)
    S.memset("dve", RSTb, 0.0)
    S.memset("dve", SST, 0.0)
    S.memset("dve", SSTb, 0.0)
    Hr = Ring(_ring(S, "H", 2, [64, BT], F32))
    hstate = [dcol[0:64, 5:6]]

    NB = 2
    PB = 1 if proj is not None else 2
    mk = lambda name, shape, dt=F32, n=NB: Ring(_ring(S, name, n, shape, dt))
    CQr, CKVr = mk("CQ", [128, 2, BT], n=1), mk("CKV", [128, BT], n=PB)
    KRr, KRRr = mk("KR", [48, BT], n=1), mk("KRR", [48, BT], n=1)
    Zr, XCr, BCr, CCr = mk("Z", [64, BT], n=1), mk("XC", [64, BT + 3], n=PB), mk("BC", [128, BT + 3], n=PB), mk("CC", [128, BT + 3], n=PB)
    DTr = mk("DT", [1, BT], n=PB)
    RQr, RQRr, RKr, RKRr = (mk(n, [64, BT], n=1) for n in ("RQ", "RQR", "RK", "RKR"))
    RVr, RGr = mk("RV", [64, BT], n=PB), mk("RG", [64, BT], n=PB)
    LXr, LGr = mk("LX", [64, BT + 3], n=PB), mk("LG", [64, BT], n=1)
    TRr, TMr = mk("TR", [64, 2, BT], n=PB), mk("TM", [48, 2, BT], n=1)
    w1 = lambda name, shape, dt=F32: S.sb(name, shape, dt)
    sqb = Ring(_ring(S, "sqb", 2, [128, BT], BF16))
    rnorm = Ring(_ring(S, "rnorm", PB, [128, BT], F32))
    cqn = w1("cqn", [128, 2, BT], BF16)
    ckvn = w1("ckvn", [128, BT], BF16)
    QT = w1("QT", [48, BT], BF16)
    rt1, rt2 = w1("rt1", [48, BT]), w1("rt2", [48, BT])
    PTr = Ring(_ring(S, "PT", 3, [128, BT], BF16))
    rden = w1("rden", [128, BT])
    YA = Ring(_ring(S, "YA", 1, [64, BT], F32))
    f64 = Ring(_ring(S, "f64", 6, [64, BT], F32))
    f128 = Ring(_ring(S, "f128", 2, [128, BT], F32))
    QRb, QDb, KRb = w1("QRb", [64, BT], BF16), w1("QDb", [64, BT], BF16), w1("KRb", [64, BT], BF16)
    XS, XSr = w1("XS", [64, BT]), None
    BS, CS = w1("BS", [128, BT], BF16), w1("CS", [128, BT], BF16)
    Ub = w1("Ub", [64, BT], BF16)
    small = Ring(_ring(S, "small", 8, [128, 8], F32))
    sq128 = Ring(_ring(S, "sq128", 4, [128, 128], F32))
    sq128b = Ring(_ring(S, "sq128b", 6, [128, 128], BF16))
    tm64b = Ring(_ring(S, "tm64b", 4, [128, 64], BF16))
    tmA = Ring(_ring(S, "tmA", 4, [128, 64], BF16))
    scA = Ring(_ring(S, "scA", 2, [128, 128], BF16))
    Of, Ob, Osq = w1("Of", [64, BT]), w1("Ob", [64, BT], BF16), w1("Osq", [64, BT], BF16)
    YBr, YCr, YDr = mk("YB", [64, BT], n=1), mk("YC", [64, BT], n=1), mk("YD", [64, BT], n=1)

    psS = Ring(_ring(S, "psS", 2, [128, 512], F32, ps=True))
    psO = S.ps("psO")
    psGl = _ring(S, "psG", 3, [128, 512], F32, ps=True)
    psG = Ring(psGl)
    psA_ = psGl[0]
    psB = Ring(psGl[1:3])
    psY = S.ps("psY")
    psR_ = S.ps("psRet")

    def conv4(dst, src, wc, bc, p):
        S.ts("dve", dst, src[0:p, 3:BT + 3], col(wc + 3, 0, p), ALU.mult, col(bc, 0, p), ALU.add)
        for j in (2, 1, 0):
            S.stt(dst, src[0:p, j:BT + j], col(wc + j, 0, p), dst, ALU.mult, ALU.add)

    HPREV = []
    ydeps = [Dep(f"ys{i}") for i in range(NBLK)]
    for bi in range(NBLK):
        t0 = bi * BT
        if ycc is not None and bi > 0:
            ycc(bi - 1, ydeps[bi - 1])
        ts_ = slice(t0, t0 + BT)
        CQ, CKV, KR, KRR = CQr.next(), CKVr.next(), KRr.next(), KRRr.next()
        Z, XC, BC, CC, DT = Zr.next(), XCr.next(), BCr.next(), CCr.next(), DTr.next()
        RQ, RQR, RK, RKR, RV, RG = RQr.next(), RQRr.next(), RKr.next(), RKRr.next(), RVr.next(), RGr.next()
        LX, LG, TR, TM = LXr.next(), LGr.next(), TRr.next(), TMr.next()
        if proj is None:
            hload(CQ[:, 0, :], "cq0", 128, t0, t0 + BT)
            hload(CQ[:, 1, :], "cq1", 128, t0, t0 + BT)
            hload(CKV, "ckv", 128, t0, t0 + BT)
            hload(KR[32:48, :], "kr", 16, t0, t0 + BT)
            hload(KRR[32:48, :], "krr", 16, t0, t0 + BT)
            hload(Z, "z", 64, t0, t0 + BT)
            hload(DT, "dt", 1, t0, t0 + BT)
            for tl, key, p in ((XC, "x", 64), (BC, "B", 128), (CC, "C", 128), (LX, "lx", 64)):
                if bi == 0:
                    S.memset("pool", tl[:, 0:3], 0.0)
                    hload(tl[:, 3:BT + 3], key, p, 0, BT)
                else:
                    hload(tl, key, p, t0 - 3, t0 + BT)
            for tl, key in ((RQ, "rq"), (RQR, "rqr"), (RK, "rk"), (RKR, "rkr"), (RV, "rv"), (RG, "rg"), (LG, "lg")):
                hload(tl, key, 64, t0, t0 + BT)
        else:
            q_ = t0 // QT_
            lc = t0 - q_ * QT_
            xbk = xbkr.next()
            for kc in range(8):
                S.dma(xbk[kc][0:64, :], xball[2 * kc, q_ * 64:(q_ + 1) * 64, lc:lc + BT])
                S.dma(xbk[kc][64:128, :], xball[2 * kc + 1, q_ * 64:(q_ + 1) * 64, lc:lc + BT])

            def evac(dst, srcp):
                e = "act" if evn[0] % 2 == 0 else "dve"
                evn[0] += 1
                S.copy(e, dst, srcp)
            plan = [
                [(CQ[:, 0, :], 0, 128)], [(CQ[:, 1, :], 0, 128)], [(CKV, 0, 128)],
                [(DT, 0, 1), (KR[32:48, :], 32, 16), (KRR[32:48, :], 64, 16)],
                [(Z, 0, 64), (XC[:, 3:BT + 3], 64, 64)], [(BC[:, 3:BT + 3], 0, 128)], [(CC[:, 3:BT + 3], 0, 128)],
                [(RQ, 0, 64), (RQR, 64, 64)], [(RK, 0, 64), (RKR, 64, 64)], [(RV, 0, 64), (RG, 64, 64)],
                [(LX[:, 3:BT + 3], 0, 64), (LG, 64, 64)],
            ]
            for tl in (XC, BC, CC, LX):
                if bi == 0:
                    S.memset("pool", tl[:, 0:3], 0.0)
                else:
                    S.copy("dve", tl[:, 0:3], HPREV[0][:, BT:BT + 3])
                    HPREV.pop(0)
            for ch, parts in enumerate(plan):
                pp = psG.next()
                for k in range(8):
                    S.mm(pp, Wh[k][:, ch * 128:(ch + 1) * 128], xbk[k], k == 0, k == 7)
                for dst, p0, n in parts:
                    evac(dst, pp[p0:p0 + n, :])
            HPREV.extend([XC, BC, CC, LX])
        S.dma(TR[:, 0, :], tabs[0:64, ts_])
        S.dma(TR[:, 1, :], tabs[64:128, ts_])
        S.dma(TM[32:48, 0, :], tabs[128:144, ts_])
        S.dma(TM[32:48, 1, :], tabs[144:160, ts_])
        cosR, sinR = TR[:, 0, :], TR[:, 1, :]
        cosM, sinM = TM[32:48, 0, :], TM[32:48, 1, :]

        if 'mla' in SECT:
            pass
            pq = psG.next()
            for k in range(2):
                sq = sqb.next()
                S.act(sq, CQ[:, k, :], AF.Square)
                S.mm(pq, ones_b, sq, k == 0, k == 1)
            rq_ = rnorm.next()
            S.act(rq_, pq, AF.Sqrt, bias=eps_col, scale=1.0 / 256)
            recip(S, rq_, rq_)
            for k in range(2):
                S.stt(cqn[:, k, :], CQ[:, k, :], col(k), rq_, ALU.mult, ALU.mult)
            pk = psG.next()
            sq = sqb.next()
            S.act(sq, CKV, AF.Square)
            S.mm(pk, ones_b, sq, True, True)
            rk_ = rnorm.next()
            S.act(rk_, pk, AF.Sqrt, bias=eps_col, scale=1.0 / 128)
            recip(S, rk_, rk_)
            S.stt(ckvn, CKV, col(2), rk_, ALU.mult, ALU.mult)
            pq = psG.next()
            for k in range(2):
                S.mm(pq[0:48, :], wts[:, W_Q + 48 * k:W_Q + 48 * (k + 1)], cqn[:, k, :], k == 0, k == 1)
            pqr = psG.next()
            for k in range(2):
                S.mm(pqr[0:48, :], wts[:, W_QR + 48 * k:W_QR + 48 * (k + 1)], cqn[:, k, :], k == 0, k == 1)
            S.copy("act", QT[0:32, :], pq[0:32, :])
            S.tt("dve", rt1[32:48, :], pq[32:48, :], cosM, ALU.mult)
            S.tt("dve", rt2[32:48, :], pqr[32:48, :], sinM, ALU.mult)
            S.tt("pool", QT[32:48, :], rt1[32:48, :], rt2[32:48, :], ALU.add)
            KTb = T(KT.ap[:, ts_], KTd[bi])
            pk = psG.next()
            S.mm(pk[0:32, :], wts[:, W_K:W_K + 32], ckvn, True, True)
            S.copy("act", KTb[0:32, :], pk[0:32, :])
            S.tt("dve", rt1[32:48, :], KR[32:48, :], cosM, ALU.mult)
            S.tt("pool", rt2[32:48, :], KRR[32:48, :], sinM, ALU.mult)
            S.tt("dve", KTb[32:48, :], rt1[32:48, :], rt2[32:48, :], ALU.add)
            pv = psG.next()
            for j in range(4):
                S.mm(pv[:, j * 64:(j + 1) * 64], ckvn[:, j * 128:(j + 1) * 128], wts[:, W_V:W_V + 64], True, True)
            V1b = T(V1.ap[:, 4 * bi:4 * bi + 4, 0:64], V1d[bi])
            S.copy("act", V1b, pv[:, 0:256].re("p (j e) -> p j e", j=4))

        def sec_ret():
            yield
            if 'ret' not in SECT:
                return
            a1, a2 = f64.next(), f64.next()
            S.tt("dve", a1, RQ, cosR, ALU.mult)
            S.tt("pool", a2, RQR, sinR, ALU.mult)
            S.tt("dve", a1, a1, a2, ALU.add)
            S.copy("pool", QRb, a1)
            for j in range(4):
                S.tt("dve" if j % 2 == 0 else "pool", QDb[:, j * 128:(j + 1) * 128], a1[:, j * 128:(j + 1) * 128], qdec, ALU.mult)
            b1, b2 = f64.next(), f64.next()
            S.tt("dve", b1, RK, cosR, ALU.mult)
            S.tt("pool", b2, RKR, sinR, ALU.mult)
            S.tt("dve", b1, b1, b2, ALU.add)
            S.act(KRb, b1, AF.Copy, scale=0.125)
            if 'noretloop' not in SECT:
                for j in range(4):
                    cj = slice(j * 128, (j + 1) * 128)
                    pt = psA_
                    ptb = T(pt.ap.bitcast(BF16), pt.dep)
                    S.transpose(ptb[:, 0:64], KRb[:, cj], identb[0:64, 0:64])
                    KD = tmA.next()
                    S.ts("dve", KD, ptb[:, 0:64], col(32), ALU.mult)
                    S.transpose(pt[:, 256:320], RV[:, cj], ident[0:64, 0:64])
                    Vt = tmA.next()
                    S.copy("act", Vt, pt[:, 256:320])
                    S.mm(pt[:, 384:512], KRb[:, cj], QRb[:, cj], True, True)
                    yield
                    sc = scA.next()
                    S.tt("dve", sc, pt[:, 384:512], intraT, ALU.mult)
                    S.mm(psR_[0:64, cj], Vt, sc, True, False)
                    S.mm(psR_[0:64, cj], RSTb, QDb[:, cj], False, True)
                    yield
                    S.mm(pt[0:64, 128:192], KD, Vt, True, True)
                    S.stt(RST, RST, col(31, 0, 64), pt[0:64, 128:192], ALU.mult, ALU.add)
                    S.copy("pool", RSTb, RST)
                    yield
            if 'noretgn' not in SECT:
                S.copy("act", Of, psR_[0:64, :])
                S.copy("pool", Ob, Of)
                S.act(Osq, Of, AF.Square)
                p1, p2 = psR_, psA_
                S.mm(p1[0:64, :], ones_b[0:64, 0:64], Ob, True, True)
                S.mm(p2[0:64, :], ones_b[0:64, 0:64], Osq, True, True)
                mean, msq, var = f64.next(), f64.next(), f64.next()
                S.ts("dve", mean, p1[0:64, :], 1.0 / 64, ALU.mult)
                S.tt("pool", msq, mean, mean, ALU.mult)
                S.stt(var, p2[0:64, :], 1.0 / 64, msq, ALU.mult, ALU.subtract)
                S.act(var, var, AF.Sqrt, bias=eps_col[0:64, :], scale=1.0)
                recip(S, var, var)
                S.tt("dve", Of, Of, mean, ALU.subtract)
                S.tt("pool", Of, Of, var, ALU.mult)
                S.act(Of, Of, AF.Identity, bias=col(22, 0, 64), scale=col(21, 0, 64))
                sg = f64.next()
                S.act(sg, RG, AF.Silu)
                yc = YCr.next()
                S.tt("dve", yc, Of, sg, ALU.mult)
                S.dma(T(yT.ap[bi, 128:192, :], ydeps[bi]) if proj is not None else yT[128:192, ts_], yc)

        def sec_ssd():
            yield
            if 'ssd' not in SECT:
                return
            xa = f128.next()[0:64, :]
            conv4(xa, XC, 3, 7, 64)
            S.act(XS, xa, AF.Silu)
            ba = f128.next()
            conv4(ba, BC, 8, 12, 128)
            S.act(BS, ba, AF.Silu)
            ca = f128.next()
            conv4(ca, CC, 13, 17, 128)
            S.act(CS, ca, AF.Silu)
            S.act(DT, DT, AF.Exp, bias=col(18, 0, 1), scale=1.0)
            S.act(DT, DT, AF.Ln, bias=one_col[0:1, :], scale=1.0)
            pd = psB.next()
            for j in range(4):
                S.mm(pd[:, j:j + 1], DT[0:1, j * 128:(j + 1) * 128], one11, True, True)
            sm = small.next()
            dt_tm, a_tm, acs, nacs, wcol, dte = sm[:, 0:4], None, None, None, None, None
            S.copy("dve", dt_tm, pd[:, 0:4])
            sm2 = small.next()
            a_tm = sm2[:, 0:4]
            S.ts("dve", a_tm, dt_tm, A_col, ALU.mult)
            S.mm(pd[:, 8:12], triu, a_tm, True, True)
            nacs = sm2[:, 4:8]
            S.ts("dve", nacs, pd[:, 8:12], -1.0, ALU.mult)
            sm3 = small.next()
            dte, wcol = sm3[:, 0:4], sm3[:, 4:8]
            for j in range(4):
                cj = slice(j * 128, (j + 1) * 128)
                abc = sq128.next()
                S.copy("pool", abc, a_tm[:, j:j + 1].bc([128, 128]))
                pr = psB.next()
                S.mm(pr[:, 0:128], abc, triu, True, True)
                S.mm(pr[:, 128:256], abc, triu, True, False)
                S.mm(pr[:, 128:256], ident, negm, False, True)
                yield
                E = sq128.next()
                S.act(E, pr[:, 0:128], AF.Exp)
                seg = sq128b.next()
                S.act(seg, pr[:, 128:256], AF.Exp, bias=nacs[:, j:j + 1], scale=1.0)
                S.act(dte[:, j:j + 1], pr[:, 127:128], AF.Exp, bias=nacs[:, j:j + 1], scale=1.0)
                S.tt("dve", wcol[:, j:j + 1], dte[:, j:j + 1], dt_tm[:, j:j + 1], ALU.mult)
                S.mm(pr[:, 256:384], BS[:, cj], CS[:, cj], True, True)
                MT = sq128b.next()
                S.tt("dve", MT, pr[:, 256:384], seg, ALU.mult)
                S.transpose(pr[:, 384:448], XS[:, cj], ident[0:64, 0:64])
                yield
                XDT, XDD = tm64b.next(), tm64b.next()
                S.ts("dve", XDT, pr[:, 384:448], dt_tm[:, j:j + 1], ALU.mult)
                S.ts("dve", XDD, pr[:, 384:448], wcol[:, j:j + 1], ALU.mult)
                pb = psB.next()
                pbb = T(pb.ap.bitcast(BF16), pb.dep)
                S.transpose(pbb[:, 0:128], BS[:, cj], identb)
                yield
                Bt = sq128b.next()
                S.copy("act", Bt, pbb[:, 0:128])
                Cs = sq128b.next()
                S.tt("pool", Cs, CS[:, cj], E, ALU.mult)
                S.mm(psY[0:64, cj], XDT, MT, True, False)
                S.mm(psY[0:64, cj], SSTb, Cs, False, True)
                yield
                S.mm(pb[:, 256:320], Bt, XDD, True, True)
                S.stt(SST, SST, E[:, 127:128], pb[:, 256:320], ALU.mult, ALU.add)
                S.copy("pool", SSTb, SST)
                yield
            yb = YBr.next()
            S.stt(yb, XS, col(20, 0, 64), psY[0:64, :], ALU.mult, ALU.add)
            sz = f128.next()[0:64, :]
            S.act(sz, Z, AF.Silu)
            S.tt("dve", yb, yb, sz, ALU.mult)
            S.dma(T(yT.ap[bi, 64:128, :], ydeps[bi]) if proj is not None else yT[64:128, ts_], yb)

        def sec_lru():
            yield
            if 'lru' not in SECT:
                return
            U = f64.next()
            conv4(U, LX, 23, 27, 64)
            S.copy("pool", Ub, U)
            pr = psA_
            S.mm(pr[0:64, :], wts[0:64, W_A:W_A + 64], Ub, True, True)
            pi = psR_
            S.mm(pi[0:64, :], wts[0:64, W_X:W_X + 64], Ub, True, True)
            rr, ii = f64.next(), f64.next()
            S.act(rr, pr[0:64, :], AF.Sigmoid, bias=col(28, 0, 64), scale=1.0)
            S.act(ii, pi[0:64, :], AF.Sigmoid, bias=col(29, 0, 64), scale=1.0)
            aa, a2_ = f64.next(), f64.next()
            S.act(aa, rr, AF.Exp, scale=cneg)
            S.act(a2_, rr, AF.Exp, scale=cneg2)
            S.ts("dve", a2_, a2_, -1.0, ALU.mult, 1.0, ALU.add)
            S.act(a2_, a2_, AF.Sqrt)
            S.tt("pool", ii, ii, U, ALU.mult)
            S.tt("dve", ii, ii, a2_, ALU.mult)
            yield
            H = Hr.next()
            S.scan(H, aa, ii, hstate[0])
            hstate[0] = H[:, BT - 1:BT]
            g1, g2 = f64.next(), f64.next()
            S.act(g1, LG, AF.Square)
            S.ts("dve", g1, g1, 0.044715, ALU.mult, 1.0, ALU.add)
            S.tt("pool", g1, g1, LG, ALU.mult)
            S.act(g2, g1, AF.Sigmoid, scale=2.0 * math.sqrt(2.0 / math.pi))
            S.tt("pool", g2, g2, LG, ALU.mult)
            yd = YDr.next()
            S.tt("dve", yd, H, g2, ALU.mult)
            S.dma(T(yT.ap[bi, 192:256, :], ydeps[bi]) if proj is not None else yT[192:256, ts_], yd)
        def stream_a():
            yield from sec_ret()
            yield from sec_lru()
        gens = [stream_a(), sec_ssd()]

        def pump(n):
            for _ in range(n):
                for g in list(gens):
                    try:
                        next(g)
                        break
                    except StopIteration:
                        gens.remove(g)
                else:
                    return
                gens.append(gens.pop(0))
        if 'mla' in SECT:
            pass
            nkb = 4 * bi + 4

            def emit_scores(kb):
                j = kb - 4 * bi
                qs = 0 if j < 0 else j * 128
                Kk = T(KT.ap[:, kb * 128:(kb + 1) * 128], KTd[kb // 4])
                ps = psS.next()
                S.mm(ps[:, qs:BT], Kk, QT[:, qs:BT], True, True)
                return ps
            nxt = emit_scores(0)
            for kb in range(nkb):
                j = kb - 4 * bi
                qs = 0 if j < 0 else j * 128
                Vk = T(V1.ap[:, kb, :], V1d[kb // 4])
                ps = nxt
                if kb + 1 < nkb:
                    nxt = emit_scores(kb + 1)
                PT = PTr.next()
                S.act(PT[:, qs:BT], ps[:, qs:BT], AF.Exp, scale=MLA_SCALE)
                if j >= 0:
                    S.tt("pool", PT[:, qs:qs + 128], PT[:, qs:qs + 128], tri01, ALU.mult)
                S.mm(psO[:, qs:BT], Vk, PT[:, qs:BT], kb == 0, kb == nkb - 1)
                if nkb < 40:
                    pump(2 if nkb < 24 else 1)
                elif (kb * 40) // nkb != ((kb + 1) * 40) // nkb:
                    pump(1)
            recip(S, rden[64:128, :], psO[64:128, :])
            ya = YA.next()
            S.tt("dve", ya, psO[0:64, :], rden[64:128, :], ALU.mult)
            S.dma(T(yT.ap[bi, 0:64, :], ydeps[bi]) if proj is not None else yT[0:64, ts_], ya)

        while gens:
            pump(1)
    if ycc is not None:
        ycc(NBLK - 1, ydeps[NBLK - 1])
    S.close_scope()


def build_tables(ntok=TOK):
    S = Sched()
    posr = S.dram("posr", [128, ntok], I32, "ExternalInput")
    fr = S.dram("fr", [128, 4], F32, "ExternalInput")
    tabs = S.dram("tabs", [160, ntok], F32, "ExternalOutput")
    emit_tables(S, posr, fr, tabs, ntok)
    return S.finish()


def emit_tables(S, posr, fr, tabs, ntok):
    S.open_scope()
    frs = S.sb("frs", [128, 4], F32)
    S.dma(frs, fr)
    CH = min(2048, ntok)
    TWO_PI = 2.0 * math.pi
    pir = Ring(_ring(S, "pi", 2, [128, CH], I32))
    pf = S.sb("pf", [128, CH], F32)
    angr = Ring(_ring(S, "ang", 2, [128, CH], F32))
    kf = S.sb("kf", [128, CH], F32)
    ki = S.sb("ki", [128, CH], I32)
    for c0 in range(0, ntok, CH):
        pi_ = pir.next()
        S.dma(pi_, posr[:, c0:c0 + CH])
        S.copy("dve", pf, pi_)
        for ti, (r0, half, sbase) in enumerate(((0, 64, 64), (128, 16, 32))):
            ang = angr.next()
            S.ts("dve", ang, pf, frs[:, ti:ti + 1], ALU.mult, frs[:, 2 + ti:3 + ti], ALU.add)
            S.ts("dve", kf, ang, 1.0 / TWO_PI, ALU.mult)
            S.copy("dve", ki, kf)
            S.copy("dve", kf, ki)
            S.stt(ang, kf, -TWO_PI, ang, ALU.mult, ALU.add)
            S.ts("dve", kf, ang, math.pi, ALU.is_gt, -TWO_PI, ALU.mult)
            S.tt("dve", ang, ang, kf, ALU.add)
            S.ts("dve", kf, ang, -math.pi, ALU.is_lt, TWO_PI, ALU.mult)
            S.tt("dve", ang, ang, kf, ALU.add)
            S.ts("dve", ang, ang, math.pi, ALU.min, -math.pi, ALU.max)
            S.act(ang, ang, AF.Sin)
            S.dma(tabs[r0:r0 + half, c0:c0 + CH], ang[0:half, :])
            S.dma(tabs[r0 + half:r0 + 2 * half, c0:c0 + CH], ang[sbase:sbase + half, :])
    S.close_scope()


def colvec(v):
    return np.ascontiguousarray(np.asarray(v, np.float32).reshape(-1, 128).T)


def swap_halves(a, width):
    sh = a.shape
    a = a.reshape(sh[:-1] + (sh[-1] // width, 2, width // 2))
    return np.ascontiguousarray(a[..., ::-1, :].reshape(sh))


def make_w_in_aug(w_in_l):
    w = np.zeros((1024, NHP), np.float32)
    w[:, :N_IN] = w_in_l
    w[:, 2964:2980] = swap_halves(w_in_l[:, 384:400], 16)
    w[:, 2980:3236] = swap_halves(w_in_l[:, 1428:1684], 64)
    w[:, 3236:3492] = swap_halves(w_in_l[:, 1684:1940], 64)
    return w


def head_rows(c):
    g = c // 2
    rows = []
    rows += list(range(0, 256)) + list(range(256, 384)) + list(range(384, 400)) + list(range(2964, 2980))
    rows += list(range(400 + 64 * c, 464 + 64 * c))
    rows += list(range(656 + 64 * c, 720 + 64 * c))
    rows += list(range(912 + 128 * g, 1040 + 128 * g))
    rows += list(range(1168 + 128 * g, 1296 + 128 * g))
    rows += [1424 + c] + [3500 + i for i in range(7)]
    rows += list(range(1428 + 64 * c, 1492 + 64 * c))
    rows += list(range(2980 + 64 * c, 3044 + 64 * c))
    rows += list(range(1684 + 64 * c, 1748 + 64 * c))
    rows += list(range(3236 + 64 * c, 3300 + 64 * c))
    rows += list(range(1940 + 64 * c, 2004 + 64 * c))
    rows += list(range(2196 + 64 * c, 2260 + 64 * c))
    rows += list(range(2452 + 64 * c, 2516 + 64 * c))
    rows += list(range(2708 + 64 * c, 2772 + 64 * c))
    assert len(rows) == NR
    return np.array(rows)


def rep(v, n=128):
    return np.full((n,), v, np.float32)


def mixer_consts(p, l, c):
    g = c // 2
    cst = np.zeros((128, NCST), np.float32)
    cst[:, 0] = p["mla_g_q"][l, 0:128]
    cst[:, 1] = p["mla_g_q"][l, 128:256]
    cst[:, 2] = p["mla_g_kv"][l]
    xs = slice(64 * c, 64 * c + 64)
    bs = slice(256 + 128 * g, 384 + 128 * g)
    cs_ = slice(512 + 128 * g, 640 + 128 * g)
    for j in range(4):
        cst[0:64, 3 + j] = p["ssd_conv_w"][l, j, xs]
        cst[:, 8 + j] = p["ssd_conv_w"][l, j, bs]
        cst[:, 13 + j] = p["ssd_conv_w"][l, j, cs_]
        cst[0:64, 23 + j] = p["lru_conv_w"][l, j, xs]
    cst[0:64, 7] = p["ssd_conv_b"][l, xs]
    cst[:, 12] = p["ssd_conv_b"][l, bs]
    cst[:, 17] = p["ssd_conv_b"][l, cs_]
    cst[:, 18] = rep(p["ssd_dt_bias"][l, c])
    cst[:, 19] = rep(p["ssd_a_log"][l, c])
    cst[:, 20] = rep(p["ssd_d"][l, c])
    cst[0:64, 21] = p["ret_gn_g"][l, xs]
    cst[0:64, 22] = p["ret_gn_b"][l, xs]
    cst[0:64, 27] = p["lru_conv_b"][l, xs]
    cst[0:64, 28] = p["lru_b_a"][l, xs]
    cst[0:64, 29] = p["lru_b_x"][l, xs]
    cst[0:64, 30] = p["lru_a_param"][l, xs]
    lg = np.log1p(-(2.0 ** (-5.0 - c)))
    idx = np.arange(128, dtype=np.float64)
    cst[:, 31] = np.exp(lg * 128)
    cst[:, 32] = np.exp(lg * (127.0 - idx))
    rel = idx[None, :] - idx[:, None]
    cst[:, C_INTRA:C_INTRA + 128] = np.where(rel >= 0, np.exp(lg * np.maximum(rel, 0.0)), 0.0)
    cst[0:64, C_QDEC:C_QDEC + 128] = np.exp(lg * (idx + 1.0))[None, :]
    cst[:, C_TRIU:C_TRIU + 128] = (rel >= 0)
    cst[:, C_NEG:C_NEG + 128] = np.where(rel >= 0, 0.0, -30000.0)
    cst[:, C_ID:C_ID + 128] = np.eye(128)
    wts = np.zeros((128, NWTS), np.float32)
    wq = p["mla_w_uq"][l][:, 48 * c:48 * c + 48]
    wqr = np.zeros_like(wq)
    wqr[:, 32:48] = swap_halves(wq[:, 32:48], 16)
    for k in range(2):
        wts[:, W_Q + 48 * k:W_Q + 48 * (k + 1)] = wq[128 * k:128 * (k + 1)]
        wts[:, W_QR + 48 * k:W_QR + 48 * (k + 1)] = wqr[128 * k:128 * (k + 1)]
    wkv = p["mla_w_ukv"][l][:, 96 * c:96 * c + 96]
    wts[:, W_K:W_K + 32] = wkv[:, 0:32]
    wts[:, W_V:W_V + 64] = wkv[:, 32:96]
    wts[0:64, W_A:W_A + 64] = p["lru_w_a"][l, c]
    wts[0:64, W_X:W_X + 64] = p["lru_w_x"][l, c]
    wts[:, W_TRI:W_TRI + 128] = (rel >= 0)
    wts[:, W_ID:W_ID + 128] = np.eye(128)
    wts[:, W_ONES:W_ONES + 128] = 1.0
    return cst, wts


def table_freqs():
    fr = np.zeros((128, 4), np.float32)
    r = np.arange(64)
    inv = (10000.0 ** (-(2.0 * (r % 32)).astype(np.float32) / 64.0)).astype(np.float32)
    fr[0:64, 0] = inv
    fr[64:128, 0] = np.where(r < 32, -inv, inv)
    r = np.arange(16)
    inv = (10000.0 ** (-(2.0 * (r % 8)).astype(np.float32) / 16.0)).astype(np.float32)
    fr[0:16, 1] = inv
    fr[32:48, 1] = np.where(r < 8, -inv, inv)
    fr[0:64, 2] = math.pi / 2
    fr[0:16, 3] = math.pi / 2
    return fr


def _run(nc, in_maps):
    res = run_bass_kernel_spmd(nc, in_maps, core_ids=list(range(8)))
    return res.results


STAGE = 99


def w_out_perm(w_out_l):
    idx = np.empty(1024, np.int64)
    for k in range(8):
        c, h = k // 2, k % 2
        for p_ in range(128):
            idx[k * 128 + p_] = (2 * h + p_ // 64) * 256 + c * 64 + p_ % 64
    return np.ascontiguousarray(w_out_l[idx])


def c_vecs(p, l):
    v = np.zeros((128, 36), np.float32)
    v[:, 0:8], v[:, 8:16] = colvec(p["ln1_g"][l]), colvec(p["ln1_b"][l])
    v[:, 16:24], v[:, 24:32] = colvec(p["ln2_g"][l]), colvec(p["ln2_b"][l])
    for c in range(4):
        v[64:128, 32 + c] = p["ssd_norm_g"][l, 64 * c:64 * c + 64]
    return v


def make_w_head(waug, c):
    g = c // 2
    w = np.zeros((1024, NRP), np.float32)
    w[:, 0:256] = waug[:, 0:256]
    w[:, 256:384] = waug[:, 256:384]
    w[:, 384] = waug[:, 1424 + c]
    w[:, 384 + 32:384 + 48] = waug[:, 384:400]
    w[:, 384 + 64:384 + 80] = waug[:, 2964:2980]
    pairs = [(400 + 64 * c, 656 + 64 * c), None, None, (1428 + 64 * c, 2980 + 64 * c), (1684 + 64 * c, 3236 + 64 * c),
             (1940 + 64 * c, 2196 + 64 * c), (2452 + 64 * c, 2708 + 64 * c)]
    w[:, 640:768] = waug[:, 912 + 128 * g:1040 + 128 * g]
    w[:, 768:896] = waug[:, 1168 + 128 * g:1296 + 128 * g]
    for ch, pr in zip((4, 5, 6, 7, 8, 9, 10), pairs):
        if pr is None:
            continue
        w[:, ch * 128:ch * 128 + 64] = waug[:, pr[0]:pr[0] + 64]
        w[:, ch * 128 + 64:ch * 128 + 128] = waug[:, pr[1]:pr[1] + 64]
    return w


def build_fused(stage=99):
    S = Sched()
    seq, tok = SEQ, TOK
    nblk = seq // 512
    IN = "ExternalInput"
    xT = S.dram("xT", [1024, tok], F32, IN)
    posr = S.dram("posr", [128, seq], I32, IN)
    fr = S.dram("fr", [128, 4], F32, IN)
    w_head = [S.dram(f"w_head{l}", [1024, NRP], F32, IN) for l in range(DEPTH)]
    w_out = [S.dram(f"w_out{l}", [1024, 1024], F32, IN) for l in range(DEPTH)]
    w_fi = [S.dram(f"w_fi{l}", [1024, 2 * D_FF], F32, IN) for l in range(DEPTH)]
    w_fo = [S.dram(f"w_fo{l}", [D_FF, 1024], F32, IN) for l in range(DEPTH)]
    vecs = [S.dram(f"vecs{l}", [128, 36], F32, IN) for l in range(DEPTH)]
    cst = [S.dram(f"cst{l}", [128, NCST], F32, IN) for l in range(DEPTH)]
    wts = [S.dram(f"wts{l}", [128, NWTS], F32, IN) for l in range(DEPTH)]
    outT = S.dram("outT", [1024, tok], F32, "ExternalOutput")
    tabs = S.dram_internal("tabs", [160, seq], F32)
    xbs = S.dram_internal("xbs", [16, 64, tok], BF16)
    xball = S.dram_internal("xball", [16, 256, tok], BF16)
    ysend = S.dram_internal("ysend", [nblk, 256, 512], F32)
    yall = S.dram_internal("yall", [nblk, 1024, 512], F32)
    yloc = S.dram_internal("yloc", [nblk // 4, 1024, 512], F32)
    xmid = S.dram_internal("xmid", [1024, tok], F32)
    G = [[0, 1, 2, 3], [4, 5, 6, 7]]
    for j in range(16):
        S.dma(xbs[j], xT[64 * j:64 * j + 64, :], q="pool")
    for j in range(16):
        S.collective("AllGather", xball[j], xbs[j], G)
    whb = [S.dram_internal(f"whb{l}", [1024, NRP], BF16) for l in range(DEPTH)]
    wob = [S.dram_internal(f"wob{l}", [1024, 1024], BF16) for l in range(DEPTH)]
    wfib = [S.dram_internal(f"wfib{l}", [1024, 2 * D_FF], BF16) for l in range(DEPTH)]
    wfob = [S.dram_internal(f"wfob{l}", [D_FF, 1024], BF16) for l in range(DEPTH)]
    for t_ in whb + wob + wfib + wfob:
        t_.dep.bg = True
    for l in range(DEPTH):
        for k in range(8):
            S.dma(whb[l][k * 128:(k + 1) * 128, :], w_head[l][k * 128:(k + 1) * 128, :], q="pool")
        for k in range(8):
            S.dma(wob[l][k * 128:(k + 1) * 128, :], w_out[l][k * 128:(k + 1) * 128, :], q="pool")
        for k in range(8):
            for j in range(4):
                S.dma(wfib[l][k * 128:(k + 1) * 128, j * 1408:(j + 1) * 1408],
                      w_fi[l][k * 128:(k + 1) * 128, j * 1408:(j + 1) * 1408], q="pool")
        for f in range(22):
            S.dma(wfob[l][f * 128:(f + 1) * 128, :], w_fo[l][f * 128:(f + 1) * 128, :], q="pool")
    emit_tables(S, posr, fr, tabs, seq)
    xcur = xT
    for l in range(DEPTH):
        if l > 0:
            for j in range(16):
                S.collective("AllGather", xball[j], xbs[j], G)

        def ycc(j, dep):
            S.collective("AllGather", T(yall.ap[j], Dep(f"ya{j}")), T(ysend.ap[j], dep), G)
        emit_mixer(S, None, tabs, cst[l], wts[l], ysend, seq, False, proj=(xball, whb[l]), ycc=ycc, wq="sp")
        yall4 = yall.re("(q t) r c -> q t r c", q=4)
        yall_t = T(yall.ap, Dep("yall_g"))
        nq = (seq // 512) // 4
        for t_ in range(nq):
            S.dma(T(yloc.ap[t_:t_ + 1], yloc.dep), yall_t,
                  dyn_in=lambda eng, t_=t_: yall4.ap[bass.ds(S.rv(eng, "c"), 1), t_])
        last = l == DEPTH - 1
        xdst = outT if last else xmid
        emit_token_c(S, xcur, None, wob[l], wfib[l], wfob[l], vecs[l], xdst, False,
                     yget=lambda k, ts, n: yloc[ts // 512, k * 128:(k + 1) * 128, ts % 512:ts % 512 + n],
                     xbs=None if last else xbs, wq="sp")
        xcur = xdst
    return S.finish()


def kernel(**inputs):
    p = {k: np.asarray(v) for k, v in inputs.items()}
    x = p["x"].astype(np.float32, copy=False)
    pos = p["positions"].astype(np.int32, copy=False)
    cores = [(b, q) for b in range(BATCH) for q in range(4)]
    tsl = lambda q: slice(q * TOK, (q + 1) * TOK)
    shared = {"fr": table_freqs()}
    waug = []
    for l in range(DEPTH):
        waug.append(make_w_in_aug(p["w_in"][l]))
        shared[f"w_out{l}"] = w_out_perm(p["w_out"][l])
        shared[f"w_fi{l}"] = np.ascontiguousarray(p["w_ffn_in"][l])
        shared[f"w_fo{l}"] = np.ascontiguousarray(p["w_ffn_out"][l])
        shared[f"vecs{l}"] = c_vecs(p, l)
    posr = [np.ascontiguousarray(np.broadcast_to(pos[b][None, :], (128, SEQ))) for b in range(BATCH)]
    in_maps = []
    for b, k in cores:
        m = dict(shared)
        m["xT"] = np.ascontiguousarray(x[b, tsl(k)].T)
        m["posr"] = posr[b]
        for l in range(DEPTH):
            m[f"cst{l}"], m[f"wts{l}"] = mixer_consts(p, l, k)
            m[f"w_head{l}"] = make_w_head(waug[l], k)
        in_maps.append(m)
    r = _run(build_fused(), in_maps)
    out = np.empty((BATCH, SEQ, D_MODEL), np.float32)
    for i, (b, q) in enumerate(cores):
        out[b, tsl(q)] = r[i]["outT"].T
    return out
```
